# Optimizing a Trainium2 kernel written in Bass

```python
import jax, jax.numpy as jnp
from jax import lax
import numpy as np

D_MODEL = 1024
BATCH = 4
SEQ = 8192
DEPTH = 2
DEC_BATCH = 128
DEC_SEQ = 8
PAST_LEN = 16384
PAGE_SIZE = 128

N_EVEN = (DEPTH + 1) // 2
N_ODD = DEPTH // 2
CHUNK = 128
A_GROUPS = 4
A_GD = 128
A_WIDTH = A_GROUPS * A_GD
POOL_WINDOWS = (2, 4, 8, 16)
B_GD = 128
B_WIDTH = len(POOL_WINDOWS) * B_GD
POOL_CTX = max(POOL_WINDOWS) - 1
AB_IN = 2 * A_WIDTH + B_WIDTH
AB_OUT = A_WIDTH + B_WIDTH
N_HEADS = 16
N_KV = 4
HEAD_DIM = 64
GQA = N_HEADS // N_KV
WINDOW = 128
ROT_DIM = HEAD_DIM // 4
ROPE_THETA = 500000.0
QKV_OUT = (N_HEADS + 2 * N_KV) * HEAD_DIM
N_MEM = 256
MEM_HEADS = 4
MEM_HD = 128
MEM_WIDTH = MEM_HEADS * MEM_HD
D_FF = 2816
CONV_W = 3
EPS = 1e-6

kernel_name = "hybrid_gmlp_pool_swa_mem_convffn_step"


def rmsnorm(x, g):
    xf = x.astype(jnp.float32)
    y = xf * lax.rsqrt(jnp.mean(xf * xf, -1, keepdims=True) + EPS)
    return (y * g.astype(jnp.float32)).astype(x.dtype)


def layernorm(x, g):
    xf = x.astype(jnp.float32)
    xc = xf - jnp.mean(xf, -1, keepdims=True)
    y = xc * lax.rsqrt(jnp.mean(xc * xc, -1, keepdims=True) + EPS)
    return (y * g.astype(jnp.float32)).astype(x.dtype)


def gelu(x):
    return jax.nn.gelu(x, approximate=False)


def rope_partial(x, pos):
    half = ROT_DIM // 2
    inv = ROPE_THETA ** (-jnp.arange(half, dtype=jnp.float32) / half)
    ang = pos.astype(jnp.float32)[:, None] * inv[None, :]
    cos = jnp.cos(ang)[None, :, None, :]
    sin = jnp.sin(ang)[None, :, None, :]
    xr = x[..., :ROT_DIM].astype(jnp.float32)
    x1, x2 = xr[..., :half], xr[..., half:]
    rot = jnp.concatenate([x1 * cos - x2 * sin, x2 * cos + x1 * sin], -1).astype(x.dtype)
    return jnp.concatenate([rot, x[..., ROT_DIM:]], -1)


def mix_ab(h, pool_ctx, pos, w_in, v_gain, w_s, b_s, pool_w, pool_scale, w_out):
    N, T, _ = h.shape
    proj = h @ w_in
    u = gelu(proj[..., :A_WIDTH])
    v = layernorm(gelu(proj[..., A_WIDTH:2 * A_WIDTH]), v_gain)
    p = proj[..., 2 * A_WIDTH:]
    L = min(T, CHUNK)
    nc = T // L
    ws = jnp.where(jnp.tril(jnp.ones((L, L), bool)), w_s[:, :L, :L], 0.0)
    vc = v.reshape(N, nc, L, A_GROUPS, A_GD)
    sg = jnp.einsum('gij,ncjgd->ncigd', ws, vc) + b_s[:, :L].T[None, None, :, :, None]
    a_out = u * sg.reshape(N, T, A_WIDTH)
    p_ext = jnp.concatenate([pool_ctx, p], 1)
    cs = jnp.cumsum(p_ext.astype(jnp.float32), axis=1)
    cs = jnp.concatenate([jnp.zeros((N, 1, B_WIDTH), jnp.float32), cs], 1)
    outs = []
    for gi, w in enumerate(POOL_WINDOWS):
        sl = slice(gi * B_GD, (gi + 1) * B_GD)
        s = cs[:, POOL_CTX + 1:POOL_CTX + 1 + T, sl] - cs[:, POOL_CTX + 1 - w:POOL_CTX + 1 - w + T, sl]
        cnt = jnp.minimum(pos + 1, w).astype(jnp.float32)[None, :, None]
        outs.append(s / cnt)
    pooled = (jnp.concatenate(outs, -1) - p.astype(jnp.float32)).astype(h.dtype)
    pooled = pooled.reshape(N, T, len(POOL_WINDOWS), B_GD)
    b_out = jnp.einsum('ntgd,gde->ntge', pooled, pool_w).reshape(N, T, B_WIDTH) * pool_scale
    y = jnp.concatenate([a_out, b_out], -1) @ w_out
    return y, p_ext[:, -POOL_CTX:], v


def sink_attend(q, k, v, mask, sinks):
    s = jnp.einsum('...qkgd,...skd->...kgqs', q, k).astype(jnp.float32) * (HEAD_DIM ** -0.5)
    s = jnp.where(mask, s, -jnp.inf)
    sink = jnp.broadcast_to(sinks.reshape(N_KV, GQA, 1, 1).astype(jnp.float32), s.shape[:-1] + (1,))
    pr = jax.nn.softmax(jnp.concatenate([s, sink], -1), axis=-1)[..., :-1]
    return jnp.einsum('...kgqs,...skd->...qkgd', pr.astype(v.dtype), v)


def swa_qkv(h, pos, w_qkv, q_gain, k_gain):
    N, T, _ = h.shape
    qkv = h @ w_qkv
    q = qkv[..., :N_HEADS * HEAD_DIM].reshape(N, T, N_HEADS, HEAD_DIM)
    k = qkv[..., N_HEADS * HEAD_DIM:(N_HEADS + N_KV) * HEAD_DIM].reshape(N, T, N_KV, HEAD_DIM)
    v = qkv[..., (N_HEADS + N_KV) * HEAD_DIM:].reshape(N, T, N_KV, HEAD_DIM)
    q = rope_partial(rmsnorm(q, q_gain), pos)
    k = rope_partial(rmsnorm(k, k_gain), pos)
    return q, k, v


def swa_prompt(h, pos, w_qkv, q_gain, k_gain, sinks, w_o):
    N, T, _ = h.shape
    q, k, v = swa_qkv(h, pos, w_qkv, q_gain, k_gain)
    nb = T // WINDOW
    qb = q.reshape(N, nb, WINDOW, N_KV, GQA, HEAD_DIM)
    kb = k.reshape(N, nb, WINDOW, N_KV, HEAD_DIM)
    vb = v.reshape(N, nb, WINDOW, N_KV, HEAD_DIM)
    pad = ((0, 0), (1, 0), (0, 0), (0, 0), (0, 0))
    kk = jnp.concatenate([jnp.pad(kb, pad)[:, :-1], kb], 2)
    vv = jnp.concatenate([jnp.pad(vb, pad)[:, :-1], vb], 2)
    i = jnp.arange(WINDOW)[:, None]
    s = jnp.arange(2 * WINDOW)[None, :]
    band = (s > i) & (s <= i + WINDOW)
    valid = band[None] & ((jnp.arange(nb)[:, None, None] > 0) | (s[None] >= WINDOW))
    o = sink_attend(qb, kk, vv, valid[:, None, None], sinks)
    y = o.reshape(N, T, N_HEADS * HEAD_DIM) @ w_o
    return y, k[:, -WINDOW:], v[:, -WINDOW:]


def swa_sample(h, k_ctx, v_ctx, pos, w_qkv, q_gain, k_gain, sinks, w_o):
    N, T, _ = h.shape
    q, k, v = swa_qkv(h, pos, w_qkv, q_gain, k_gain)
    kk = jnp.concatenate([k_ctx, k], 1)
    vv = jnp.concatenate([v_ctx, v], 1)
    i = jnp.arange(T)[:, None]
    s = jnp.arange(WINDOW + T)[None, :]
    mask = (s > i) & (s <= i + WINDOW)
    o = sink_attend(q.reshape(N, T, N_KV, GQA, HEAD_DIM), kk, vv, mask, sinks)
    y = o.reshape(N, T, N_HEADS * HEAD_DIM) @ w_o
    return y, kk[:, -WINDOW:], vv[:, -WINDOW:]


def mem_kv(mem, g_mem, w_kv, k_gain):
    N = mem.shape[0]
    m = rmsnorm(mem, g_mem) @ w_kv
    k = rmsnorm(m[..., :MEM_WIDTH].reshape(N, N_MEM, MEM_HEADS, MEM_HD), k_gain)
    v = m[..., MEM_WIDTH:].reshape(N, N_MEM, MEM_HEADS, MEM_HD)
    return k, v


def mem_attend(h, k, v, w_q, q_gain, w_o):
    N, T, _ = h.shape
    q = rmsnorm((h @ w_q).reshape(N, T, MEM_HEADS, MEM_HD), q_gain)
    s = jnp.einsum('nthd,nmhd->nhtm', q, k).astype(jnp.float32) * (MEM_HD ** -0.5)
    pr = jax.nn.softmax(s, axis=-1).astype(v.dtype)
    o = jnp.einsum('nhtm,nmhd->nthd', pr, v)
    return o.reshape(N, T, MEM_WIDTH) @ w_o


def conv_ffn(h, g_ctx, w_up, conv_w, conv_b, w_down):
    T = h.shape[1]
    up = h @ w_up
    g, u = up[..., :D_FF], up[..., D_FF:]
    g_ext = jnp.concatenate([g_ctx, g], 1)
    gc = conv_b
    for j in range(CONV_W):
        gc = gc + conv_w[j] * g_ext[:, j:j + T]
    y = (gelu(gc) * u) @ w_down
    return y, g_ext[:, -(CONV_W - 1):]


def setup_inputs(seed: int = 0) -> dict:
    key = jax.random.key(seed)
    ks = iter(jax.random.split(key, 48))

    def nrm(shape, scale):
        return jax.random.normal(next(ks), shape, jnp.float32) * scale

    def gain(shape):
        return 1.0 + nrm(shape, 0.02)

    D = D_MODEL
    return {
        "x_prompt": nrm((BATCH, SEQ, D), 1.0),
        "x_sample": nrm((DEC_BATCH, DEC_SEQ, D), 1.0),
        "cache_pool": nrm((N_EVEN, DEC_BATCH, POOL_CTX, B_WIDTH), 1.0),
        "cache_swa_k": nrm((N_ODD, DEC_BATCH, WINDOW, N_KV, HEAD_DIM), 1.0),
        "cache_swa_v": nrm((N_ODD, DEC_BATCH, WINDOW, N_KV, HEAD_DIM), 1.0),
        "cache_mem_k": nrm((DEPTH, DEC_BATCH, N_MEM, MEM_HEADS, MEM_HD), 1.0),
        "cache_mem_v": nrm((DEPTH, DEC_BATCH, N_MEM, MEM_HEADS, MEM_HD), 1.0),
        "cache_ffn_conv": nrm((DEPTH, DEC_BATCH, CONV_W - 1, D_FF), 1.0),
        "mem_prompt": nrm((BATCH, N_MEM, D), 1.0),
        "ln_mix": gain((DEPTH, D)),
        "ln_mem": gain((DEPTH, D)),
        "ln_memkv": gain((DEPTH, D)),
        "ln_ffn": gain((DEPTH, D)),
        "ab_w_in": nrm((N_EVEN, D, AB_IN), D ** -0.5),
        "ab_v_gain": gain((N_EVEN, A_WIDTH)),
        "ab_w_s": nrm((N_EVEN, A_GROUPS, CHUNK, CHUNK), CHUNK ** -0.5),
        "ab_b_s": 1.0 + nrm((N_EVEN, A_GROUPS, CHUNK), 0.1),
        "ab_pool_w": nrm((N_EVEN, len(POOL_WINDOWS), B_GD, B_GD), B_GD ** -0.5),
        "ab_pool_scale": 1.0 + nrm((N_EVEN, B_WIDTH), 0.1),
        "ab_w_out": nrm((N_EVEN, AB_OUT, D), AB_OUT ** -0.5),
        "c_w_qkv": nrm((N_ODD, D, QKV_OUT), D ** -0.5),
        "c_q_gain": gain((N_ODD, HEAD_DIM)),
        "c_k_gain": gain((N_ODD, HEAD_DIM)),
        "c_sinks": nrm((N_ODD, N_HEADS), 0.5),
        "c_w_o": nrm((N_ODD, N_HEADS * HEAD_DIM, D), (N_HEADS * HEAD_DIM) ** -0.5),
        "m_w_q": nrm((DEPTH, D, MEM_WIDTH), D ** -0.5),
        "m_w_kv": nrm((DEPTH, D, 2 * MEM_WIDTH), D ** -0.5),
        "m_q_gain": gain((DEPTH, MEM_HD)),
        "m_k_gain": gain((DEPTH, MEM_HD)),
        "m_w_o": nrm((DEPTH, MEM_WIDTH, D), MEM_WIDTH ** -0.5),
        "f_w_up": nrm((DEPTH, D, 2 * D_FF), D ** -0.5),
        "f_conv_w": nrm((DEPTH, CONV_W, D_FF), CONV_W ** -0.5),
        "f_conv_b": nrm((DEPTH, D_FF), 0.02),
        "f_w_down": nrm((DEPTH, D_FF, D), D_FF ** -0.5),
    }


def reference(x_prompt, x_sample, cache_pool, cache_swa_k, cache_swa_v, cache_mem_k, cache_mem_v,
              cache_ffn_conv, mem_prompt, ln_mix, ln_mem, ln_memkv, ln_ffn, ab_w_in, ab_v_gain,
              ab_w_s, ab_b_s, ab_pool_w, ab_pool_scale, ab_w_out, c_w_qkv, c_q_gain, c_k_gain,
              c_sinks, c_w_o, m_w_q, m_w_kv, m_q_gain, m_k_gain, m_w_o, f_w_up, f_conv_w,
              f_conv_b, f_w_down):
    xp, xs = x_prompt, x_sample
    Bp, Tp, _ = xp.shape
    Bs, Ts, _ = xs.shape
    pos_p = jnp.arange(Tp)
    pos_s = PAST_LEN + jnp.arange(Ts)
    pool_p, pool_s, chunk_v_s = [], [], []
    swa_kp, swa_vp, swa_ks, swa_vs = [], [], [], []
    mem_kp, mem_vp, conv_p, conv_s = [], [], [], []
    for l in range(DEPTH):
        j = l // 2
        hp = rmsnorm(xp, ln_mix[l])
        hs = rmsnorm(xs, ln_mix[l])
        if l % 2 == 0:
            zp = jnp.zeros((Bp, POOL_CTX, B_WIDTH), xp.dtype)
            yp, stp, _ = mix_ab(hp, zp, pos_p, ab_w_in[j], ab_v_gain[j], ab_w_s[j], ab_b_s[j],
                                ab_pool_w[j], ab_pool_scale[j], ab_w_out[j])
            ys, sts, vs = mix_ab(hs, cache_pool[j], pos_s, ab_w_in[j], ab_v_gain[j], ab_w_s[j],
                                 ab_b_s[j], ab_pool_w[j], ab_pool_scale[j], ab_w_out[j])
            pool_p.append(stp)
            pool_s.append(sts)
            chunk_v_s.append(vs)
        else:
            yp, kp, vp = swa_prompt(hp, pos_p, c_w_qkv[j], c_q_gain[j], c_k_gain[j], c_sinks[j], c_w_o[j])
            ys, ks_, vs_ = swa_sample(hs, cache_swa_k[j], cache_swa_v[j], pos_s, c_w_qkv[j],
                                      c_q_gain[j], c_k_gain[j], c_sinks[j], c_w_o[j])
            swa_kp.append(kp)
            swa_vp.append(vp)
            swa_ks.append(ks_)
            swa_vs.append(vs_)
        xp = xp + yp
        xs = xs + ys
        mk, mv = mem_kv(mem_prompt, ln_memkv[l], m_w_kv[l], m_k_gain[l])
        mem_kp.append(mk)
        mem_vp.append(mv)
        xp = xp + mem_attend(rmsnorm(xp, ln_mem[l]), mk, mv, m_w_q[l], m_q_gain[l], m_w_o[l])
        xs = xs + mem_attend(rmsnorm(xs, ln_mem[l]), cache_mem_k[l], cache_mem_v[l], m_w_q[l],
                             m_q_gain[l], m_w_o[l])
        zc = jnp.zeros((Bp, CONV_W - 1, D_FF), xp.dtype)
        fp, cp = conv_ffn(rmsnorm(xp, ln_ffn[l]), zc, f_w_up[l], f_conv_w[l], f_conv_b[l], f_w_down[l])
        fs, cs = conv_ffn(rmsnorm(xs, ln_ffn[l]), cache_ffn_conv[l], f_w_up[l], f_conv_w[l],
                          f_conv_b[l], f_w_down[l])
        conv_p.append(cp)
        conv_s.append(cs)
        xp = xp + fp
        xs = xs + fs
    return (xp, xs,
            jnp.stack(pool_p), jnp.stack(pool_s), jnp.stack(chunk_v_s),
            jnp.stack(swa_kp), jnp.stack(swa_vp), jnp.stack(swa_ks), jnp.stack(swa_vs),
            jnp.stack(mem_kp), jnp.stack(mem_vp),
            jnp.stack(conv_p), jnp.stack(conv_s))
```

```python
import numpy as np
import concourse.bass as bass
import concourse.mybir as mybir
from concourse.bass_utils import run_bass_kernel_spmd

F32 = mybir.dt.float32
BF16 = mybir.dt.bfloat16
AF = mybir.ActivationFunctionType
ALU = mybir.AluOpType
AX = mybir.AxisListType

D = 1024
KC = 8
DFF = 2816
NFC = 22
PAST_LEN = 16384
EPS = 1e-6
TW = 512
HW = 256


class Res:
    __slots__ = ("name", "lw", "rd", "dsem", "dcnt", "is_dram", "owner", "excl", "dq")

    def __init__(self, name):
        self.name = name
        self.is_dram = False
        self.owner = None
        self.excl = False
        self.dq = {}
        self.lw = None
        self.rd = set()
        self.dsem = None
        self.dcnt = 0


class V:
    __slots__ = ("ap", "res")

    def __init__(self, ap, res):
        self.ap = ap
        self.res = res if isinstance(res, (list, tuple)) else [res]


class T:
    def __init__(self, handle, shape, name, dram=False):
        self.h = handle
        self.shape = list(shape)
        self.name = name
        self.dram = dram
        self.res = Res(name)
        self.res.is_dram = dram
        self.res.owner = self
        self.sub = {}
        self.F = int(np.prod(shape[1:]))

    def d(self, dims, off=0, key=None):
        ap = bass.AP(tensor=self.h, offset=off, ap=[list(x) for x in dims])
        if not getattr(self, "track", False):
            return V(ap, [])
        return V(ap, [self.r(key)])

    def r(self, key=None):
        if key is None:
            return self.res
        if key not in self.sub:
            self.sub[key] = Res("%s/%s" % (self.name, key))
        return self.sub[key]

    def v(self, dims=None, off=0, p0=0, np_=None, key=None):
        if np_ is None:
            np_ = self.shape[0] - p0
        if dims is None:
            dims = [[1, self.F]]
        ap = bass.AP(tensor=self.h, offset=p0 * self.F + off, ap=[[self.F, np_]] + [list(d) for d in dims])
        if isinstance(key, list):
            res = [self.r(k) for k in key]
        else:
            res = self.r(key)
        return V(ap, res)


class Op:
    __slots__ = ("eng", "fn", "reads", "writes", "dma_res", "deps", "pos", "signal", "waits", "sigval", "dma_val", "dma_q")

    def __init__(self, eng, fn, reads, writes, dma_res=None):
        self.eng = eng
        self.fn = fn
        self.reads = reads
        self.writes = writes
        self.dma_res = dma_res
        self.signal = False
        self.waits = []


ENGS = ("pe", "act", "dve", "pool", "sp")


class Prog:
    def __init__(self, nc):
        self.nc = nc
        self.ops = []

    def add(self, eng, fn, reads, writes, dma_res=None):
        rr = []
        ww = []
        for v in reads:
            rr.extend(v.res)
            if eng != "pe":
                for r in v.res:
                    if r.excl:
                        ww.append(r)
        for v in writes:
            ww.extend(v.res)
        self.ops.append(Op(eng, fn, rr, ww, dma_res))

    def mm(self, out, lhsT, rhs, start=True, stop=True):
        self.add("pe", lambda e: e.matmul(out.ap, lhsT.ap, rhs.ap, start=start, stop=stop), [lhsT, rhs], [out])

    def tr(self, out, in_, ident):
        self.add("pe", lambda e: e.transpose(out.ap, in_.ap, ident.ap), [in_, ident], [out])

    def act(self, out, in_, func, bias=None, scale=None, accum=None, eng="act"):
        kw = {}
        rd = [in_]
        wr = [out]
        if bias is not None:
            kw["bias"] = bias.ap if isinstance(bias, V) else bias
            if isinstance(bias, V):
                rd.append(bias)
        if scale is not None:
            kw["scale"] = scale.ap if isinstance(scale, V) else scale
            if isinstance(scale, V):
                rd.append(scale)
        if accum is not None:
            kw["accum_out"] = accum.ap
            wr.append(accum)
        self.add(eng, lambda e: e.activation(out.ap, in_.ap, func, **kw), rd, wr)

    def tt(self, out, a, b, op, eng="dve"):
        self.add(eng, lambda e: e.tensor_tensor(out.ap, a.ap, b.ap, op), [a, b], [out])

    def ts(self, out, a, s1, s2, op0, op1=None, eng="dve"):
        rd = [a]
        x1 = s1.ap if isinstance(s1, V) else s1
        x2 = s2.ap if isinstance(s2, V) else s2
        if isinstance(s1, V):
            rd.append(s1)
        if isinstance(s2, V):
            rd.append(s2)
        if op1 is None:
            self.add(eng, lambda e: e.tensor_scalar(out.ap, a.ap, x1, None, op0), rd, [out])
        else:
            self.add(eng, lambda e: e.tensor_scalar(out.ap, a.ap, x1, x2, op0, op1), rd, [out])

    def stt(self, out, a, s, b, op0, op1, eng="dve"):
        rd = [a, b]
        xs = s.ap if isinstance(s, V) else s
        if isinstance(s, V):
            rd.append(s)
        self.add(eng, lambda e: e.scalar_tensor_tensor(out.ap, a.ap, xs, b.ap, op0, op1), rd, [out])

    def red(self, out, a, op=ALU.add, eng="dve"):
        self.add(eng, lambda e: e.tensor_reduce(out.ap, a.ap, AX.X, op), [a], [out])

    def recip(self, out, a):
        self.add("dve", lambda e: e.reciprocal(out.ap, a.ap), [a], [out])

    def copy(self, out, a, eng="dve"):
        if eng == "act":
            self.add("act", lambda e: e.activation(out.ap, a.ap, AF.Copy), [a], [out])
        else:
            self.add(eng, lambda e: e.tensor_copy(out.ap, a.ap), [a], [out])

    def memset(self, out, val, eng="dve"):
        self.add(eng, lambda e: e.memset(out.ap, val), [], [out])

    def bn(self, mv, a, st):
        self.add("dve", lambda e: e.bn_stats(st.ap, a.ap), [a], [st])
        self.add("dve", lambda e: e.bn_aggr(mv.ap, st.ap), [st], [mv])

    def dma(self, out, in_, eng="pool"):
        key = None
        for v in (out, in_):
            for r in v.res:
                key = r
                break
            if key is not None:
                break
        if key is None:
            if not hasattr(self, "d2d"):
                self.d2d = Res("d2d")
            key = self.d2d
        self.add(eng, lambda e: e.dma_start(out=out.ap, in_=in_.ap), [in_], [out], dma_res=key)

    def finalize(self):
        ops = self.ops
        pos_ctr = {e: 0 for e in ENGS}
        def expand(lst):
            out = []
            for r in lst:
                out.append(r)
                if r.owner is not None:
                    out.extend(r.owner.sub.values())
            return out

        for i, op in enumerate(ops):
            op.reads = expand(op.reads)
            op.writes = expand(op.writes)
            deps = set()
            for r in op.reads:
                if r.lw is not None:
                    deps.add(r.lw)
            for r in op.writes:
                if r.lw is not None:
                    deps.add(r.lw)
                deps |= r.rd
            for r in op.reads:
                r.rd.add(i)
            for r in op.writes:
                r.lw = i
                r.rd = set()
            deps.discard(i)
            op.deps = deps
            pos_ctr[op.eng] += 1
            op.pos = pos_ctr[op.eng]
            if op.dma_res is not None:
                q = op.dma_res.dq.setdefault(op.eng, [None, 0])
                q[1] += 16
                op.dma_val = q[1]
                op.dma_q = q
        known = {e: {e2: 0 for e2 in ENGS} for e in ENGS}
        kdma = {e: {} for e in ENGS}
        snaps = [None] * len(ops)
        for i, op in enumerate(ops):
            E = op.eng
            kn = known[E]
            need_c = {}
            need_d = {}
            for j in op.deps:
                pj = ops[j]
                if pj.dma_res is not None:
                    r = pj.dma_q
                    if kdma[E].get(id(r), 0) < pj.dma_val:
                        if need_d.get(id(r), (None, 0))[1] < pj.dma_val:
                            need_d[id(r)] = (r, pj.dma_val)
                else:
                    if pj.eng == "pe" and E == "pe":
                        continue
                    if kn[pj.eng] < pj.pos:
                        if pj.eng not in need_c or ops[need_c[pj.eng]].pos < pj.pos:
                            need_c[pj.eng] = j
            for e2, j in need_c.items():
                pj = ops[j]
                pj.signal = True
                op.waits.append(("c", j))
                sn = snaps[j]
                for e3 in ENGS:
                    if sn[e3] > kn[e3]:
                        kn[e3] = sn[e3]
                if kn[e2] < pj.pos:
                    kn[e2] = pj.pos
            for _, (r, val) in need_d.items():
                op.waits.append(("d", r, val))
                kdma[E][id(r)] = val
            sn = dict(kn)
            if op.dma_res is None and E != "sp":
                pass
            snaps[i] = sn
        sig_ctr = {e: 0 for e in ENGS}
        for op in ops:
            if op.dma_res is None and op.signal:
                sig_ctr[op.eng] += 1
                op.sigval = sig_ctr[op.eng]
        return sig_ctr

    def emit(self, stack):
        nc = self.nc
        ops = self.ops
        self.finalize()
        csem = {e: stack.enter_context(nc.semaphore("c_" + e)) for e in ENGS}
        dres = []
        for op in ops:
            if op.dma_res is not None and op.dma_q[0] is None:
                op.dma_q[0] = stack.enter_context(nc.semaphore("d%d" % len(dres)))
                dres.append(op.dma_q)
        block = stack.enter_context(nc.Block())
        by_eng = {e: [op for op in ops if op.eng == e] for e in ENGS}

        def run(e, eng_obj):
            for op in by_eng[e]:
                ws = []
                for w in op.waits:
                    if w[0] == "c":
                        pj = ops[w[1]]
                        ws.append((csem[pj.eng], pj.sigval))
                    else:
                        ws.append((w[1][0], w[2]))
                for (s, val) in ws[1:]:
                    eng_obj.wait_ge(s, val)
                ins = op.fn(eng_obj)
                if ws:
                    ins._wait_ge(ws[0][0], ws[0][1])
                if op.dma_res is not None:
                    ins.then_inc(op.dma_q[0], 16)
                elif op.signal:
                    ins.then_inc(csem[e], 1)
            if e == "sp":
                for r in dres:
                    eng_obj.wait_ge(r[0], r[1])

        @block.tensor
        def _(e):
            run("pe", e)

        @block.scalar
        def _(e):
            run("act", e)

        @block.vector
        def _(e):
            run("dve", e)

        @block.gpsimd
        def _(e):
            run("pool", e)

        @block.sync
        def _(e):
            run("sp", e)


def const_layout(NT):
    NBLK = 2 + 4 * NT + 1
    items = []
    for l in range(2):
        items += [("g_mix%d" % l, 8), ("g_mem%d" % l, 8), ("g_ffn%d" % l, 8), ("g_memkv%d" % l, 8),
                  ("mqg%d" % l, 1), ("mkg%d" % l, 128), ("convw%d" % l, 66), ("convb%d" % l, 22)]
    items += [("vgain", 512), ("bs_p", 512), ("bs_s", 512), ("pscale", 4), ("invcnt", 64), ("flag", 1),
              ("cqg", 64), ("ckg", 64), ("sinks", 16), ("cos", NBLK * 8), ("sin", NBLK * 8),
              ("m_cur", 128), ("m_prev", 128), ("m_pf", 128), ("m_sn", 128), ("m_sc", 8)]
    off = {}
    o = 0
    for k, n in items:
        off[k] = o
        o += n
    return off, o


C2 = {"wsT_p": 0, "wsT_s": 512, "poolw": 1024, "ident": 1536}
NC2 = 1664


def build(NT):
    from contextlib import ExitStack
    nc = bass.Bass("TRN2", target_bir_lowering=False)
    stack = ExitStack()
    P = Prog(nc)
    CO, NCONST = const_layout(NT)
    NTOK = NT * TW

    def dr(name, shape, kind="ExternalInput", dt=F32):
        h = nc.dram_tensor(name, list(shape), dt, kind=kind)
        return T(h, shape, name, dram=True)

    def sb(name, shape, dt=F32):
        h = stack.enter_context(nc.sbuf_tensor(name, list(shape), dt))
        return T(h, shape, name)

    def psum(name, shape, dt=F32):
        h = stack.enter_context(nc.psum_tensor(name, list(shape), dt))
        t = T(h, shape, name)
        t.res.excl = True
        return t

    d_xh = dr("xh", [HW, D]); d_xo = dr("xo", [NTOK, D]); d_xs = dr("xs", [128, D]); d_xm = dr("xm", [256, D])
    d_c = dr("consts", [128, NCONST])
    d_c2 = dr("consts2", [128, NC2])
    d_win = dr("w_in", [128, 8, 1536]); d_wout = dr("w_out", [128, 8, 1024]); d_wqkv = dr("w_qkv", [128, 8, 1536])
    d_wo = dr("w_o", [128, 8, 1024])
    d_mwq = dr("m_wq", [2, 128, 8, 512]); d_mwkv = dr("m_wkv", [2, 128, 8, 1024]); d_mwo = dr("m_wo", [2, 128, 4, 1024])
    d_fup = dr("f_up", [2, NFC, 128, 8, 256]); d_fdn = dr("f_down", [2, 128, NFC, 1024])
    twins = {}

    def twin(t):
        h = nc.dram_tensor(t.name + "_b", list(t.shape), BF16, kind="Internal")
        tw = T(h, t.shape, t.name + "_b", dram=True)
        tw.track = True
        twins[t.name] = (t, tw)
        return tw

    b_mwkv = twin(d_mwkv); b_win = twin(d_win); b_wout = twin(d_wout); b_mwq = twin(d_mwq); b_mwo = twin(d_mwo)
    b_fup = twin(d_fup); b_fdn = twin(d_fdn); b_wqkv = twin(d_wqkv); b_wo = twin(d_wo)
    d_cpool = dr("c_pool", [128, 4, 16, 15])
    d_cswak = dr("c_swak", [16, 128, 256]); d_cswav = dr("c_swav", [16, 128, 256]); d_cswakT = dr("c_swakT", [64, 4, 16, 128])
    d_cmemkT = dr("c_memkT", [2, 16, 128, 4, 256]); d_cmemv = dr("c_memv", [2, 16, 256, 512])
    d_cconv = dr("c_conv", [2, 128, NFC, 16, 2])
    O = "ExternalOutput"
    o_y = dr("y", [NTOK, D], O); o_ys = dr("ys", [128, D], O)
    o_poolp = dr("o_poolp", [128, 4, 15], O); o_pools = dr("o_pools", [128, 4, 16, 15], O); o_chunkv = dr("o_chunkv", [128, 512], O)
    o_swakp = dr("o_swakp", [128, 256], O); o_swavp = dr("o_swavp", [128, 256], O)
    o_swaks = dr("o_swaks", [16, 128, 256], O); o_swavs = dr("o_swavs", [16, 128, 256], O)
    o_memk = dr("o_memk", [2, 256, 512], O); o_memv = dr("o_memv", [2, 256, 512], O)
    o_convp = dr("o_convp", [2, 128, NFC, 2], O); o_convs = dr("o_convs", [2, 128, NFC, 16, 2], O)

    C = sb("C", [128, NCONST])
    X = sb("X", [128, 4, D])
    HT = sb("HT", [128, 8, 512], BF16)
    HN = [sb("HN%d" % i, [128, D], BF16) for i in range(2)]
    JUNK = sb("JUNK", [128, D], BF16)
    STAT = sb("STAT", [128, 256])
    IDB = sb("IDB", [128, 128], BF16); ONES = sb("ONES", [128, 128], BF16)
    MASKB = sb("MASKB", [128, 4, 128], BF16); MSCB = sb("MSCB", [128, 8], BF16)
    WSTB = sb("WSTB", [128, 8, 128], BF16); POOLWB = sb("POOLWB", [128, 4, 128], BF16)
    ESINK = sb("ESINK", [128, 16]); EPSB = sb("EPSB", [128, 1])
    MKT = [sb("MKT%d" % l, [128, 4, 256], BF16) for l in range(2)]
    MVV = [sb("MVV%d" % l, [128, 2, 512], BF16) for l in range(2)]
    CT = [sb("CT%d" % l, [128, NFC, 2]) for l in range(2)]
    CS = sb("CS", [128, 2, NFC, 16, 2])
    U = sb("U", [128, 4, 512])
    VG = sb("VG", [128, 528]); VO = sb("VO", [128, 528]); VB = sb("VB", [128, 512], BF16)
    SG = sb("SG", [128, 512])
    PEXT = sb("PEXT", [128, 4, 271]); PEXS = sb("PEXS", [128, 4, 16, 23])
    PT0 = VG; PT1 = VO
    POOLED = sb("POOLED", [128, 4, 512], BF16)
    ABT = sb("ABT", [128, 16, 512], BF16)
    QT = sb("QT", [128, 4, 512], BF16)
    KTR = sb("KTR", [64, 4, 4, 128], BF16); VVR = sb("VVR", [128, 4, 512], BF16)
    KTN = sb("KTN", [64, 4, 128], BF16); VNB = sb("VNB", [128, 256], BF16)
    PTB = [sb("PTB%d" % i, [128, 2, 512], BF16) for i in range(2)]
    RD = sb("RD", [128, 512])
    OT = sb("OT", [128, 4, 512], BF16)
    SQ = sb("SQ", [128, 512]); QF = sb("QF", [128, 512]); QB = sb("QB", [128, 512], BF16)
    RT = sb("RT", [128, 4, 64])
    HF = sb("HF", [128, 11, 512], BF16)
    GE = sb("GE", [128, 4, 260])
    ACC = sb("ACC", [128, 4, 256])
    GL = sb("GL", [128, 4, 256], BF16)
    NS = 6
    RING = [sb("RING%d" % i, [128, 4096], BF16) for i in range(NS)]
    PS = [psum("PS%d" % i, [128, 512]) for i in range(6)]
    TB = [psum("TB%d" % i, [128, 1024], BF16) for i in range(2)]
    st = {"ps": 0, "tb": 0, "ring": 0, "stat": 0, "hn": 0, "ge": 0}

    def nextps():
        st["ps"] = (st["ps"] + 1) % 6
        return PS[st["ps"]]

    def nexttb():
        st["tb"] = (st["tb"] + 1) % 2
        return TB[st["tb"]]

    def statcol(n=1):
        c = st["stat"]
        if (c % 16) + n > 16:
            c = (c // 16 + 1) * 16
        if c + n > 256:
            c = 0
        st["stat"] = c + n
        return c

    def sv(c, n=1, dims=None):
        return STAT.v(dims if dims is not None else [[1, n]], off=c, key=c // 16)

    def wload(src, np_=128, eng="sp"):
        st["ring"] = (st["ring"] + 1) % NS
        slot = RING[st["ring"]]
        n = 1
        dims = []
        shp = src.ap.shape[1:]
        for s_ in shp:
            n *= s_
        stride = n
        for s_ in shp:
            stride //= s_
            dims.append([stride, s_])
        P.dma(slot.v(dims, np_=np_), src, eng=eng)
        return slot

    def cv(name, dims, add=0, np_=128):
        return C.v(dims, off=CO[name] + add, np_=np_)

    P.dma(C.v(), d_c.d([[NCONST, 128], [1, NCONST]]))
    P.dma(U.v([[1, NC2]]), d_c2.d([[NC2, 128], [1, NC2]]))
    c2 = lambda name, dims: U.v(dims, off=C2[name])
    PCHAIN = Res("pchain")

    def prepass(name, off, n, key):
        src, dst = twins[name]
        o = dst.d([[n, 1], [1, n]], off=off, key=key)
        o.res.append(PCHAIN)
        P.dma(o, src.d([[n, 1], [1, n]], off=off), eng="pool")

    LW = {"m_wkv": 128 * 8 * 1024, "m_wq": 128 * 8 * 512, "m_wo": 128 * 4 * 1024, "f_up": NFC * 128 * 2048, "f_down": 128 * NFC * 1024}
    pre_early = [lambda l=l: prepass("m_wkv", l * LW["m_wkv"], LW["m_wkv"], l) for l in range(2)]
    pre_late = [lambda: prepass("w_in", 0, 128 * 8 * 1536, 0), lambda: prepass("w_out", 0, 128 * 8 * 1024, 0)]
    for l in range(2):
        if l == 1:
            pre_late.append(lambda: prepass("w_qkv", 0, 128 * 8 * 1536, 0))
            pre_late.append(lambda: prepass("w_o", 0, 128 * 8 * 1024, 0))
        pre_late.append(lambda l=l: prepass("m_wq", l * LW["m_wq"], LW["m_wq"], l))
        pre_late.append(lambda l=l: prepass("m_wo", l * LW["m_wo"], LW["m_wo"], l))
        pre_late.append(lambda l=l: prepass("f_up", l * LW["f_up"], 11 * 128 * 2048, (l, 0)))
        pre_late.append(lambda l=l: prepass("f_down", l * LW["f_down"], LW["f_down"], l))
        pre_late.append(lambda l=l: prepass("f_up", l * LW["f_up"] + 11 * 128 * 2048, 11 * 128 * 2048, (l, 1)))
    P.copy(IDB.v(), c2("ident", [[1, 128]]))
    P.memset(ONES.v(), 1.0)
    P.memset(EPSB.v(), EPS)
    for i, nm in enumerate(["m_cur", "m_prev", "m_pf", "m_sn"]):
        P.copy(MASKB.v([[1, 128]], off=i * 128), cv(nm, [[1, 128]]))
    P.copy(MSCB.v(), cv("m_sc", [[1, 8]]))
    P.tt(WSTB.v([[128, 4], [1, 128]]), c2("wsT_p", [[128, 4], [1, 128]]), cv("m_cur", [[0, 4], [1, 128]]), ALU.mult)
    P.tt(WSTB.v([[128, 4], [1, 128]], off=512), c2("wsT_s", [[128, 4], [1, 128]]), cv("m_sn", [[0, 4], [1, 128]]), ALU.mult)
    P.copy(POOLWB.v(), c2("poolw", [[1, 512]]))
    P.act(ESINK.v(), cv("sinks", [[1, 16]]), AF.Exp)
    P.memset(PEXT.v(), 0.0)
    P.memset(KTR.v(), 0.0)
    P.memset(VVR.v(), 0.0)
    for l in range(2):
        P.memset(CT[l].v(), 0.0)
    P.dma(CS.v([[NFC * 32, 2], [1, NFC * 32]]), d_cconv.d([[NFC * 32, 128], [128 * NFC * 32, 2], [1, NFC * 32]]))
    P.dma(X.v([[1, 960]]), d_cpool.d([[960, 128], [1, 960]]))
    P.copy(PEXS.v([[23, 64], [1, 15]]), X.v([[15, 64], [1, 15]]))
    P.dma(o_swaks.d([[128 * 256, 16], [1, 120 * 256]]), d_cswak.d([[128 * 256, 16], [1, 120 * 256]], off=8 * 256))
    P.dma(o_swavs.d([[128 * 256, 16], [1, 120 * 256]]), d_cswav.d([[128 * 256, 16], [1, 120 * 256]], off=8 * 256))

    def xk(nb):
        return list(range(nb))

    wcache = {}
    st["ringctr"] = 0

    def wget(pair, key, fn):
        key = (pair, key)
        ent = wcache.get(key)
        if ent is not None and st["ringctr"] - ent[1] < NS - 1:
            return ent[0]
        slot = fn()
        if not isinstance(slot, list):
            wcache[key] = (slot, st["ringctr"])
        else:
            wcache[key] = (slot, st["ringctr"] - len(slot) + 1)
        return slot

    def wl(src, np_=128, eng="sp"):
        st["ringctr"] += 1
        return wload(src, np_=np_, eng=eng)

    def rsqrt_cols(c_in, c_out, n, scale):
        P.act(sv(c_out, n), sv(c_in, n), AF.Ln, bias=EPSB.v(), scale=scale)
        P.act(sv(c_out, n), sv(c_out, n), AF.Exp, scale=-0.5)

    def norm(tl, gname):
        nb, c0, xb0 = tl["nb"], tl["c0"], tl["xb0"]
        c = statcol(2 * nb)
        for b in range(nb):
            xb = X.v([[1, D]], off=(xb0 + b) * D, key=xb0 + b)
            P.act(JUNK.v(), xb, AF.Square, accum=sv(c + b))
        yield
        rsqrt_cols(c, c + nb, nb, 1.0 / D)
        for b in range(nb):
            xb = X.v([[1, D]], off=(xb0 + b) * D, key=xb0 + b)
            st["hn"] ^= 1
            hn = HN[st["hn"]]
            P.act(hn.v(), xb, AF.Identity, scale=sv(c + nb + b))
            tb = nexttb()
            for kc in range(8):
                P.tr(tb.v([[1, 128]], off=kc * 128), hn.v([[1, 128]], off=kc * 128), IDB.v())
            P.tt(HT.v([[512, 8], [1, 128]], off=c0 + b * 128, key=xb0 + b), tb.v([[128, 8], [1, 128]]),
                 cv(gname, [[1, 8], [0, 128]]), ALU.mult)
        yield

    def resid(tl, b, half, ps):
        xv = X.v([[1, 512]], off=(tl["xb0"] + b) * D + half * 512, key=tl["xb0"] + b)
        P.tt(xv, xv, ps.v(), ALU.add)

    def head_rstd(ps, nh, hd):
        P.act(SQ.v([[1, nh * hd]]), ps.v([[1, nh * hd]]), AF.Square)
        c = statcol(2 * nh)
        P.red(sv(c, nh), SQ.v([[hd, nh], [1, hd]]))
        rsqrt_cols(c, c + nh, nh, 1.0 / hd)
        return c + nh

    def htk(tl):
        return [tl["xb0"] + b for b in range(tl["nb"])]

    def memkv(l):
        tl = {"nb": 2, "c0": 0, "xb0": 0, "W": 256}
        if l == 0:
            for b in range(2):
                P.dma(X.v([[1, D]], off=b * D, key=b), d_xm.d([[D, 128], [1, D]], off=b * 128 * D))
            for f in pre_early:
                f()
        for _ in norm(tl, "g_memkv%d" % l):
            pass
        wk = wl(b_mwkv.d([[8 * 1024, 128], [1024, 8], [1, 512]], off=l * 128 * 8 * 1024, key=l))
        wv = wl(b_mwkv.d([[8 * 1024, 128], [1024, 8], [1, 512]], off=l * 128 * 8 * 1024 + 512, key=l))
        for b in range(2):
            ps = nextps()
            for kc in range(8):
                P.mm(ps.v(), HT.v([[1, 128]], off=kc * 512 + b * 128, key=b), wk.v([[1, 512]], off=kc * 512), kc == 0, kc == 7)
            rc = head_rstd(ps, 4, 128)
            P.tt(QF.v([[128, 4], [1, 128]]), ps.v([[128, 4], [1, 128]]), sv(rc, dims=[[1, 4], [0, 128]]), ALU.mult)
            P.tt(QF.v([[128, 4], [1, 128]]), QF.v([[128, 4], [1, 128]]), cv("mkg%d" % l, [[0, 4], [1, 128]]), ALU.mult)
            P.dma(o_memk.d([[512, 128], [1, 512]], off=(l * 256 + b * 128) * 512), QF.v())
            P.copy(QB.v(), QF.v())
            tb = nexttb()
            for h in range(4):
                P.tr(tb.v([[1, 128]], off=h * 128), QB.v([[1, 128]], off=h * 128), IDB.v())
            P.copy(MKT[l].v([[256, 4], [1, 128]], off=b * 128), tb.v([[128, 4], [1, 128]]), eng="act")
            ps = nextps()
            for kc in range(8):
                P.mm(ps.v(), HT.v([[1, 128]], off=kc * 512 + b * 128, key=b), wv.v([[1, 512]], off=kc * 512), kc == 0, kc == 7)
            P.copy(VO.v([[1, 512]]), ps.v(), eng="act")
            P.dma(o_memv.d([[512, 128], [1, 512]], off=(l * 256 + b * 128) * 512), VO.v([[1, 512]]))
            P.copy(MVV[l].v([[1, 512]], off=b * 512), VO.v([[1, 512]]))

    def mix_ab(tl):
        W, nb, kind, c0, xb0, hh = tl["W"], tl["nb"], tl["kind"], tl["c0"], tl["xb0"], tl["h"]
        sample = kind == "s"
        allb = htk(tl)
        yield from norm(tl, "g_mix0")
        wu = wget(tl.get('pair', -1), "w_in_u", lambda: wl(b_win.d([[8 * 1536, 128], [1536, 8], [1, 512]], off=0, key=0)))
        for g in range(4):
            ps = nextps()
            for kc in range(8):
                P.mm(ps.v([[1, W]]), wu.v([[1, 128]], off=kc * 512 + g * 128), HT.v([[1, W]], off=kc * 512 + c0, key=allb), kc == 0, kc == 7)
            P.act(U.v([[1, W]], off=g * 512 + c0, key=(g, hh)), ps.v([[1, W]]), AF.Gelu)
        yield
        wv = wget(tl.get('pair', -1), "w_in_v", lambda: wl(b_win.d([[8 * 1536, 128], [1536, 8], [1, 512]], off=512, key=0)))
        for b in range(nb):
            ps = nextps()
            for kc in range(8):
                P.mm(ps.v(), HT.v([[1, 128]], off=kc * 512 + c0 + b * 128, key=xb0 + b), wv.v([[1, 512]], off=kc * 512), kc == 0, kc == 7)
            P.act(VG.v([[1, 512]]), ps.v(), AF.Gelu)
            cst = statcol(9)
            P.bn(sv(cst, 2), VG.v([[1, 512]]), sv(cst + 2, 6))
            P.act(sv(cst + 8), sv(cst + 1), AF.Ln, bias=EPSB.v(), scale=1.0)
            P.act(sv(cst + 8), sv(cst + 8), AF.Exp, scale=-0.5)
            P.ts(VG.v([[1, 512]]), VG.v([[1, 512]]), sv(cst), sv(cst + 8), ALU.subtract, ALU.mult)
            P.tt(VO.v([[1, 512]]), VG.v([[1, 512]]), cv("vgain", [[1, 512]]), ALU.mult)
            P.copy(VB.v(), VO.v([[1, 512]]))
            if sample:
                P.dma(o_chunkv.d([[512, 128], [1, 512]]), VO.v([[1, 512]]))
            psg = nextps()
            for g in range(4):
                P.mm(psg.v([[1, 128]], off=g * 128), VB.v([[1, 128]], off=g * 128),
                     WSTB.v([[1, 128]], off=((4 if sample else 0) + g) * 128))
            P.tt(SG.v(), psg.v(), cv("bs_s" if sample else "bs_p", [[1, 512]]), ALU.add)
            P.tt(ABT.v([[512, 4], [1, 128]], off=c0 + b * 128, key="a%d" % (xb0 + b)), SG.v([[128, 4], [1, 128]]),
                 U.v([[512, 4], [1, 128]], off=c0 + b * 128, key=[(g, hh) for g in range(4)]), ALU.mult)
            yield
        wp = wget(tl.get('pair', -1), "w_in_p", lambda: wl(b_win.d([[8 * 1536, 128], [1536, 8], [1, 512]], off=1024, key=0)))
        def pool_b(g):
            ps2 = nextps()
            P.mm(ps2.v([[1, W]]), POOLWB.v([[1, 128]], off=g * 128), POOLED.v([[1, W]], off=g * 512 + c0, key=(g, hh)))
            P.act(ABT.v([[1, W]], off=(4 + g) * 512 + c0, key=("p", g, hh)), ps2.v([[1, W]]), AF.Identity, scale=cv("pscale", [[1, 1]], add=g))

        if sample:
            nseg, L, Tt = 16, 23, 8
            PE_, pstride = PEXS, 16 * 23
        else:
            nseg, L, Tt = 1, 15 + W, W
            PE_, pstride = PEXT, 271
        for g in range(4):
            ps = nextps()
            for kc in range(8):
                P.mm(ps.v([[1, W]]), wp.v([[1, 128]], off=kc * 512 + g * 128), HT.v([[1, W]], off=kc * 512 + c0, key=allb), kc == 0, kc == 7)
            P.copy(PE_.v([[L, nseg], [1, Tt]], off=g * pstride + 15, key=g), ps.v([[Tt, nseg], [1, Tt]]), eng="act")
            src = (PE_, g * pstride, g)
            bufs = [PT0, PT1]
            for k in range(g + 1):
                sh = 1 << k
                lo = (1 << (k + 1)) - 1
                dst = bufs[k % 2]
                s_t, s_off, s_key = src
                P.tt(dst.v([[L, nseg], [1, L - lo]], off=lo),
                     s_t.v([[L, nseg], [1, L - lo]], off=s_off + lo, key=s_key),
                     s_t.v([[L, nseg], [1, L - lo]], off=s_off + lo - sh, key=s_key), ALU.add)
                src = (dst, 0, None)
            s_t, s_off, s_key = src
            w = 1 << (g + 1)
            P.stt(POOLED.v([[Tt, nseg], [1, Tt]], off=g * 512 + c0, key=(g, hh)),
                  s_t.v([[L, nseg], [1, Tt]], off=s_off + 15, key=s_key), 1.0 / w,
                  PE_.v([[L, nseg], [1, Tt]], off=g * pstride + 15, key=g), ALU.mult, ALU.subtract)
            if tl.get("first"):
                P.tt(SG.v([[1, 16]]), s_t.v([[1, 16]], off=s_off + 15, key=s_key), cv("invcnt", [[1, 16]], add=g * 16), ALU.mult)
                P.tt(POOLED.v([[1, 16]], off=g * 512 + c0, key=(g, hh)), SG.v([[1, 16]]), PE_.v([[1, 16]], off=g * pstride + 15, key=g), ALU.subtract)
            if not sample:
                P.copy(PEXT.v([[1, 15]], off=g * 271, key=g), PEXT.v([[1, 15]], off=g * 271 + W, key=g))
                if kind == "h":
                    P.ts(PEXT.v([[1, 15]], off=g * 271, key=g), PEXT.v([[1, 15]], off=g * 271, key=g), cv("flag", [[1, 1]]), None, ALU.mult)
            yield
            if g > 0:
                pool_b(g - 1)
                yield
        pool_b(3)
        yield
        if sample:
            P.copy(U.v([[15, 64], [1, 15]]), PEXS.v([[23, 64], [1, 15]], off=8, key=[0, 1, 2, 3]))
            P.dma(o_pools.d([[960, 128], [1, 960]]), U.v([[1, 960]]))
        elif tl.get("last"):
            P.dma(o_poolp.d([[60, 128], [15, 4], [1, 15]]), PEXT.v([[271, 4], [1, 15]], key=[0, 1, 2, 3]))
        abk = ["a%d" % (xb0 + b) for b in range(nb)] + [("p", g, hh) for g in range(4)]
        for half in range(2):
            wo = wget(tl.get('pair', -1), ("w_out", half), lambda: wl(b_wout.d([[8 * 1024, 128], [1024, 8], [1, 512]], off=half * 512, key=0)))
            for b in range(nb):
                ps = nextps()
                for kc in range(8):
                    P.mm(ps.v(), ABT.v([[1, 128]], off=kc * 512 + c0 + b * 128, key=abk), wo.v([[1, 512]], off=kc * 512), kc == 0, kc == 7)
                resid(tl, b, half, ps)
            yield

    def mem_attn(tl, l):
        W, nb, kind, c0, xb0, hh = tl["W"], tl["nb"], tl["kind"], tl["c0"], tl["xb0"], tl["h"]
        sample = kind == "s"
        allb = htk(tl)
        yield from norm(tl, "g_mem%d" % l)
        wq = wget(tl.get('pair', -1), ("m_wq", l), lambda: wl(b_mwq.d([[8 * 512, 128], [512, 8], [1, 512]], off=l * 128 * 8 * 512, key=l)))
        for b in range(nb):
            ps = nextps()
            for kc in range(8):
                P.mm(ps.v(), HT.v([[1, 128]], off=kc * 512 + c0 + b * 128, key=xb0 + b), wq.v([[1, 512]], off=kc * 512), kc == 0, kc == 7)
            rc = head_rstd(ps, 4, 128)
            QBx = [QB, VB][hh]
            P.tt(QBx.v([[128, 4], [1, 128]]), ps.v([[128, 4], [1, 128]]), sv(rc, dims=[[1, 4], [0, 128]]), ALU.mult)
            yield
            tb = nexttb()
            for h in range(4):
                P.tr(tb.v([[1, 128]], off=h * 128), QBx.v([[1, 128]], off=h * 128), IDB.v())
            P.act(QT.v([[512, 4], [1, 128]], off=c0 + b * 128, key=xb0 + b), tb.v([[128, 4], [1, 128]]), AF.Identity, scale=cv("mqg%d" % l, [[1, 1]]))
            yield
        sc = 128.0 ** -0.5
        if not sample:
            for h in range(4):
                pt = PTB[h % 2]
                for mc in range(2):
                    ps = nextps()
                    P.mm(ps.v([[1, W]]), MKT[l].v([[1, 128]], off=h * 256 + mc * 128), QT.v([[1, W]], off=h * 512 + c0, key=allb))
                    P.act(pt.v([[1, W]], off=mc * 512 + c0, key=(mc, hh)), ps.v([[1, W]]), AF.Exp, scale=sc)
                yield
                pso = nextps(); psd = nextps()
                for mc in range(2):
                    P.mm(pso.v([[1, W]]), MVV[l].v([[1, 128]], off=mc * 512 + h * 128), pt.v([[1, W]], off=mc * 512 + c0, key=(mc, hh)), mc == 0, mc == 1)
                for mc in range(2):
                    P.mm(psd.v([[1, W]]), ONES.v(), pt.v([[1, W]], off=mc * 512 + c0, key=(mc, hh)), mc == 0, mc == 1)
                P.act(RD.v([[1, W]], off=c0, key=hh), psd.v([[1, W]]), AF.Ln)
                P.act(RD.v([[1, W]], off=c0, key=hh), RD.v([[1, W]], off=c0, key=hh), AF.Exp, scale=-1.0)
                P.tt(OT.v([[1, W]], off=h * 512 + c0, key=(h, hh)), pso.v([[1, W]]), RD.v([[1, W]], off=c0, key=hh), ALU.mult)
                yield
        else:
            pss = [nextps(), nextps()]
            for bb in range(16):
                kt = wl(d_cmemkT.d([[1024, 128], [1, 1024]], off=(l * 16 + bb) * 128 * 1024), eng="pool")
                for h in range(4):
                    for mc in range(2):
                        P.mm(pss[h // 2].v([[1, 8]], off=((h % 2) * 2 + mc) * 128 + bb * 8),
                             kt.v([[1, 128]], off=h * 256 + mc * 128), QT.v([[1, 8]], off=h * 512 + bb * 8, key=0))
            for i in range(2):
                P.act(PTB[i].v([[1, 512]]), pss[i].v(), AF.Exp, scale=sc)
            pso = nextps(); psd = nextps()
            for bb in range(16):
                vv = wl(d_cmemv.d([[512, 128], [128 * 512, 2], [1, 512]], off=(l * 16 + bb) * 256 * 512), eng="pool")
                for h in range(4):
                    for mc in range(2):
                        P.mm(pso.v([[1, 8]], off=h * 128 + bb * 8), vv.v([[1, 128]], off=mc * 512 + h * 128),
                             PTB[h // 2].v([[1, 8]], off=((h % 2) * 2 + mc) * 128 + bb * 8), mc == 0, mc == 1)
            for h in range(4):
                for mc in range(2):
                    P.mm(psd.v([[1, 128]], off=h * 128), ONES.v(), PTB[h // 2].v([[1, 128]], off=((h % 2) * 2 + mc) * 128), mc == 0, mc == 1)
            P.act(RD.v(), psd.v(), AF.Ln)
            P.act(RD.v(), RD.v(), AF.Exp, scale=-1.0)
            P.tt(OT.v([[512, 4], [1, 128]], key=[(h, hh) for h in range(4)]), pso.v([[128, 4], [1, 128]]), RD.v([[128, 4], [1, 128]]), ALU.mult)
        for half in range(2):
            wo = wget(tl.get('pair', -1), ("m_wo", l, half), lambda: wl(b_mwo.d([[4 * 1024, 128], [1024, 4], [1, 512]], off=l * 128 * 4 * 1024 + half * 512, key=l)))
            for b in range(nb):
                ps = nextps()
                for h in range(4):
                    P.mm(ps.v(), OT.v([[1, 128]], off=h * 512 + c0 + b * 128, key=(h, hh)), wo.v([[1, 512]], off=h * 512), h == 0, h == 3)
                resid(tl, b, half, ps)
            yield

    def ffn(tl, l):
        W, nb, kind, c0, xb0, hh = tl["W"], tl["nb"], tl["kind"], tl["c0"], tl["xb0"], tl["h"]
        sample = kind == "s"
        allb = htk(tl)
        yield from norm(tl, "g_ffn%d" % l)
        if sample:
            nseg, L, Tt = 16, 10, 8
        else:
            nseg, L, Tt = 1, 2 + W, W
        def stage1(c, ci):
            w = wget(tl.get('pair', -1), ("fup", l, c), lambda: wl(b_fup.d([[2048, 128], [1, 2048]], off=(l * NFC + c) * 128 * 2048, key=(l, c // 11))))
            psg = nextps()
            for kc in range(8):
                P.mm(psg.v([[1, W]]), w.v([[1, 128]], off=kc * 256), HT.v([[1, W]], off=kc * 512 + c0, key=allb), kc == 0, kc == 7)
            idx = hh * 2 + (c % 2)
            geo, aco = idx * 260, idx * 256
            gev = lambda dims, off=0: GE.v(dims, off=geo + off, key=idx)
            if sample:
                P.copy(gev([[L, 16], [1, 2]]), CS.v([[2, 16], [1, 2]], off=(l * NFC + c) * 32))
            else:
                P.copy(gev([[1, 2]]), CT[l].v([[1, 2]], off=c * 2, key=c))
            cw = lambda j: cv("convw%d" % l, [[1, 1]], add=c * 3 + j)
            av = ACC.v([[Tt, nseg], [1, Tt]], off=aco, key=idx)
            P.act(av, psg.v([[Tt, nseg], [1, Tt]]), AF.Identity, bias=cv("convb%d" % l, [[1, 1]], add=c), scale=cw(2))
            P.copy(gev([[L, nseg], [1, Tt]], 2), psg.v([[Tt, nseg], [1, Tt]]), eng="act")
            if sample:
                P.copy(U.v([[2, 16], [1, 2]], off=c * 32), gev([[L, 16], [1, 2]], 8))
            else:
                P.copy(CT[l].v([[1, 2]], off=c * 2, key=c), gev([[1, 2]], W))
                if kind == "h":
                    P.ts(CT[l].v([[1, 2]], off=c * 2, key=c), CT[l].v([[1, 2]], off=c * 2, key=c), cv("flag", [[1, 1]]), None, ALU.mult)
            cw = lambda j: cv("convw%d" % l, [[1, 1]], add=c * 3 + j)
            av = ACC.v([[Tt, nseg], [1, Tt]], off=aco, key=idx)
            P.stt(av, gev([[L, nseg], [1, Tt]], 1), cw(1), av, ALU.mult, ALU.add)
            P.stt(av, gev([[L, nseg], [1, Tt]], 0), cw(0), av, ALU.mult, ALU.add)
            return dict(c=c, ci=ci, w=w, idx=idx, aco=aco)

        def stage2(stt_):
            c, ci, idx, aco = stt_["c"], stt_["ci"], stt_["idx"], stt_["aco"]
            w = wget(tl.get('pair', -1), ("fup", l, c), lambda: wl(b_fup.d([[2048, 128], [1, 2048]], off=(l * NFC + c) * 128 * 2048, key=(l, c // 11))))
            psu = nextps()
            for kc in range(8):
                P.mm(psu.v([[1, W]]), w.v([[1, 128]], off=kc * 256 + 128), HT.v([[1, W]], off=kc * 512 + c0, key=allb), kc == 0, kc == 7)
            P.act(GL.v([[1, W]], off=aco, key=idx), ACC.v([[1, W]], off=aco, key=idx), AF.Gelu)
            P.tt(HF.v([[1, W]], off=ci * 512 + c0, key=(ci, hh)), GL.v([[1, W]], off=aco, key=idx), psu.v([[1, W]]), ALU.mult)

        for hf in range(2):
            prev_st = None
            for ci in range(11):
                cur_st = stage1(hf * 11 + ci, ci)
                yield
                if prev_st is not None:
                    stage2(prev_st)
                    yield
                prev_st = cur_st
            stage2(prev_st)
            yield
            wd = wget(tl.get('pair', -1), ("fdn", l, hf), lambda: [wl(b_fdn.d([[NFC * 1024, 128], [1, n * 1024]], off=l * 128 * NFC * 1024 + (hf * 11 + c0_) * 1024, key=l))
                                                for (c0_, n) in ((0, 4), (4, 4), (8, 3))])
            for b in range(nb):
                for half in range(2):
                    ps = nextps()
                    for ci in range(11):
                        P.mm(ps.v(), HF.v([[1, 128]], off=ci * 512 + c0 + b * 128, key=(ci, hh)),
                             wd[ci // 4].v([[1, 512]], off=(ci % 4) * 1024 + half * 512), ci == 0, ci == 10)
                    resid(tl, b, half, ps)
            yield
        if sample:
            P.dma(o_convs.d([[NFC * 32, 128], [1, NFC * 32]], off=l * 128 * NFC * 32), U.v([[1, NFC * 32]]))
        elif tl.get("last"):
            P.dma(o_convp.d([[NFC * 2, 128], [1, NFC * 2]], off=l * 128 * NFC * 2), CT[l].v(key=list(range(NFC))))

    OTS = ABT
    QTS = [QT, POOLED]

    def swa(tl):
        W, nb, kind, c0, xb0, hh = tl["W"], tl["nb"], tl["kind"], tl["c0"], tl["xb0"], tl["h"]
        sample = kind == "s"
        QTx = QTS[hh]
        yield from norm(tl, "g_mix1")
        wq = [wget(tl.get('pair', -1), ("w_qkv", p), lambda: wl(b_wqkv.d([[8 * 1536, 128], [1536, 8], [1, 512]], off=p * 512, key=0))) for p in range(3)]
        if sample:
            kc_ = [wl(d_cswakT.d([[4 * 16 * 128, 64], [1, 2 * 16 * 128]], off=i * 2 * 16 * 128), np_=64, eng="pool") for i in range(2)]
            vc_ = wl(d_cswav.d([[256, 128], [128 * 256, 16], [1, 256]]), eng="pool")
        for b in range(nb):
            gb = tl["gb0"] + b
            slot = gb % 4
            xb = xb0 + b
            for part in range(3):
                ps = nextps()
                for kc in range(8):
                    P.mm(ps.v(), HT.v([[1, 128]], off=kc * 512 + c0 + b * 128, key=xb), wq[part].v([[1, 512]], off=kc * 512), kc == 0, kc == 7)
                nh = 8 if part < 2 else 4
                rc = head_rstd(ps, nh, 64)
                qf = lambda dims, off=0: QF.v(dims, off=off)
                P.tt(qf([[64, nh], [1, 64]]), ps.v([[64, nh], [1, 64]]), sv(rc, dims=[[1, nh], [0, 64]]), ALU.mult)
                P.tt(qf([[64, nh], [1, 64]]), qf([[64, nh], [1, 64]]), cv("cqg" if part < 2 else "ckg", [[0, nh], [1, 64]]), ALU.mult)
                cosv = cv("cos", [[0, nh], [1, 8]], add=tl["rb0"] * 8 + b * 8)
                sinv = cv("sin", [[0, nh], [1, 8]], add=tl["rb0"] * 8 + b * 8)
                x1 = qf([[64, nh], [1, 8]], 0); x2 = qf([[64, nh], [1, 8]], 8)
                r = lambda i: RT.v([[8, nh], [1, 8]], off=i * 64, key=i)
                P.tt(r(0), x1, cosv, ALU.mult); P.tt(r(1), x2, sinv, ALU.mult)
                P.tt(r(2), x2, cosv, ALU.mult); P.tt(r(3), x1, sinv, ALU.mult)
                P.tt(x1, r(0), r(1), ALU.subtract); P.tt(x2, r(2), r(3), ALU.add)
                P.copy(QB.v([[1, nh * 64]]), QF.v([[1, nh * 64]]))
                tb = nexttb()
                for h in range(nh):
                    P.tr(tb.v([[1, 128]], off=h * 128, np_=64), QB.v([[1, 64]], off=h * 64), IDB.v())
                if part < 2:
                    P.copy(QTx.v([[1, 1024]], off=part * 1024, np_=64, key="q%d" % part), tb.v([[1, 1024]], np_=64), eng="act")
                else:
                    if sample:
                        P.copy(KTN.v([[1, 512]]), tb.v([[1, 512]], np_=64), eng="act")
                        P.copy(VNB.v(), ps.v([[1, 256]], off=256))
                        P.copy(VO.v([[1, 256]]), ps.v([[1, 256]], off=256), eng="act")
                        for bb in range(16):
                            P.dma(o_swaks.d([[256, 8], [1, 256]], off=(bb * 128 + 120) * 256), QF.v([[1, 256]], p0=bb * 8, np_=8))
                            P.dma(o_swavs.d([[256, 8], [1, 256]], off=(bb * 128 + 120) * 256), VO.v([[1, 256]], p0=bb * 8, np_=8))
                    else:
                        P.copy(KTR.v([[512, 4], [1, 128]], off=slot * 128, key=slot), tb.v([[128, 4], [1, 128]], np_=64), eng="act")
                        P.copy(VVR.v([[128, 4], [64, 2], [1, 64]], off=slot * 512, key=slot), ps.v([[64, 4], [0, 2], [1, 64]], off=256))
                        if tl.get("last") and b == nb - 1:
                            P.dma(o_swakp.d([[256, 128], [1, 256]]), QF.v([[1, 256]]))
                            P.copy(VO.v([[1, 256]]), ps.v([[1, 256]], off=256), eng="act")
                            P.dma(o_swavp.d([[256, 128], [1, 256]]), VO.v([[1, 256]]))
                yield (("have", gb) if (part == 2 and not sample) else None)
            if not sample:
                yield ("need", gb - 1)
            for kv in range(4):
                qk = "q%d" % (kv // 2)
                qv = QTx.v([[1, 512]], off=kv * 512, np_=64, key=qk)
                pso = nextps(); psd = nextps()
                if not sample:
                    first = tl.get("first") and b == 0
                    plan = [((slot + 3) % 4, 2 if first else 1), (slot, 0)]
                    pts = []
                    for j, (sl, mi) in enumerate(plan):
                        ps = nextps()
                        P.mm(ps.v(), KTR.v([[1, 128]], off=(kv * 4 + sl) * 128, key=sl), qv)
                        pt = PTB[j].v([[1, 512]], off=(kv % 2) * 512, key=kv % 2)
                        P.act(pt, ps.v(), AF.Exp, scale=0.125)
                        pt4 = PTB[j].v([[128, 4], [1, 128]], off=(kv % 2) * 512, key=kv % 2)
                        P.tt(pt4, pt4, MASKB.v([[0, 4], [1, 128]], off=mi * 128), ALU.mult)
                        pts.append((pt, sl))
                    for j, (pt, sl) in enumerate(pts):
                        P.mm(pso.v(), VVR.v([[1, 128]], off=sl * 512 + kv * 128, key=sl), pt, j == 0, j == 1)
                    for j, (pt, sl) in enumerate(pts):
                        P.mm(psd.v(), ONES.v(), pt, j == 0, j == 1)
                    P.tt(RD.v([[128, 4], [1, 128]]), psd.v([[128, 4], [1, 128]]), ESINK.v([[1, 4], [0, 128]], off=kv * 4), ALU.add)
                    P.act(RD.v(), RD.v(), AF.Ln)
                    P.act(RD.v(), RD.v(), AF.Exp, scale=-1.0)
                    for hf_ in range(2):
                        P.tt(OTS.v([[128, 2], [1, 128]], off=xb * 1024 + kv * 256, p0=64 * hf_, np_=64, key="o%d" % xb),
                             pso.v([[256, 2], [1, 128]], off=128 * hf_, p0=64 * hf_, np_=64),
                             RD.v([[256, 2], [1, 128]], off=128 * hf_, p0=64 * hf_, np_=64), ALU.mult)
                else:
                    ps = nextps()
                    P.mm(ps.v([[32, 16], [8, 4], [1, 8]]), KTN.v([[1, 128]], off=kv * 128),
                         QTx.v([[8, 16], [128, 4], [1, 8]], off=kv * 512, np_=64, key=qk))
                    ptn = PTB[0].v([[1, 512]], off=(kv % 2) * 512, key=kv % 2)
                    P.act(ptn, ps.v(), AF.Exp, scale=0.125)
                    ptn4 = PTB[0].v([[32, 16], [8, 4], [1, 8]], off=(kv % 2) * 512, key=kv % 2)
                    P.tt(ptn4, ptn4, MASKB.v([[8, 16], [0, 4], [1, 8]], off=3 * 128), ALU.mult)
                    ps2 = nextps()
                    kt = kc_[kv // 2]
                    for bb in range(16):
                        P.mm(ps2.v([[8, 4], [1, 8]], off=bb * 32), kt.v([[1, 128]], off=((kv % 2) * 16 + bb) * 128, np_=64),
                             QTx.v([[128, 4], [1, 8]], off=kv * 512 + bb * 8, np_=64, key=qk))
                    ptc = PTB[1].v([[1, 512]], off=(kv % 2) * 512, key=kv % 2)
                    P.act(ptc, ps2.v(), AF.Exp, scale=0.125)
                    ptc3 = PTB[1].v([[8, 64], [1, 8]], off=(kv % 2) * 512, key=kv % 2)
                    P.tt(ptc3, ptc3, MSCB.v([[0, 64], [1, 8]]), ALU.mult)
                    for hp in range(2):
                        P.mm(pso.v(p0=64 * hp, np_=64), VNB.v([[1, 64]], off=kv * 64), ptn, True, False)
                        for bb in range(16):
                            P.mm(pso.v([[1, 32]], off=bb * 32, p0=64 * hp, np_=64), vc_.v([[1, 64]], off=bb * 256 + kv * 64),
                                 PTB[1].v([[1, 32]], off=(kv % 2) * 512 + bb * 32, key=kv % 2), False, bb == 15)
                    P.mm(psd.v(), ONES.v(), ptn, True, False)
                    P.mm(psd.v(), ONES.v(), ptc, False, True)
                    P.tt(RD.v([[32, 16], [8, 4], [1, 8]]), psd.v([[32, 16], [8, 4], [1, 8]]),
                         ESINK.v([[0, 16], [1, 4], [0, 8]], off=kv * 4), ALU.add)
                    P.act(RD.v(), RD.v(), AF.Ln)
                    P.act(RD.v(), RD.v(), AF.Exp, scale=-1.0)
                    for hf_ in range(2):
                        P.tt(OTS.v([[8, 16], [128, 2], [1, 8]], off=xb * 1024 + kv * 256, p0=64 * hf_, np_=64, key="o%d" % xb),
                             pso.v([[32, 16], [16, 2], [1, 8]], off=8 * hf_, p0=64 * hf_, np_=64),
                             RD.v([[32, 16], [16, 2], [1, 8]], off=8 * hf_, p0=64 * hf_, np_=64), ALU.mult)
                if kv % 2 == 1:
                    yield
        for half in range(2):
            wo = wget(tl.get('pair', -1), ("w_o", half), lambda: wl(b_wo.d([[8 * 1024, 128], [1024, 8], [1, 512]], off=half * 512, key=0)))
            for b in range(nb):
                xb = xb0 + b
                ps = nextps()
                for j in range(8):
                    P.mm(ps.v(), OTS.v([[1, 128]], off=xb * 1024 + j * 128, key="o%d" % xb),
                         wo.v([[1, 512]], off=j * 512), j == 0, j == 7)
                resid(tl, b, half, ps)
            yield

    def tile_gen(tl, src, dst):
        nb, xb0 = tl["nb"], tl["xb0"]
        for b in range(nb):
            P.dma(X.v([[1, D]], off=(xb0 + b) * D, key=xb0 + b), src(b), eng="pool")
        yield from mix_ab(tl)
        yield from mem_attn(tl, 0)
        yield from ffn(tl, 0)
        yield from swa(tl)
        yield from mem_attn(tl, 1)
        yield from ffn(tl, 1)
        if dst is not None:
            for b in range(nb):
                P.dma(dst(b), X.v([[1, D]], off=(xb0 + b) * D, key=xb0 + b), eng="pool")

    LAG = _HOOK.get("lag", 3)

    have = set()

    def run_pair(ga, gb_):
        gens = [g for g in (ga, gb_) if g is not None]
        if len(gens) == 1:
            for r in gens[0]:
                if isinstance(r, tuple) and r[0] == "have":
                    have.add(r[1])
            return
        a, b = gens
        a_alive = b_alive = True
        a_n = b_n = 0
        pend = None
        while a_alive or b_alive:
            if pend is not None and (pend in have or pend < 0 or not a_alive):
                pend = None
            adv_a = a_alive and (not b_alive or pend is not None or a_n - b_n < LAG)
            if adv_a:
                r = next(a, "END")
                a_n += 1
                if r == "END":
                    a_alive = False
                elif isinstance(r, tuple) and r[0] == "have":
                    have.add(r[1])
            else:
                r = next(b, "END")
                b_n += 1
                if r == "END":
                    b_alive = False
                elif isinstance(r, tuple):
                    if r[0] == "have":
                        have.add(r[1])
                    elif r[0] == "need" and r[1] >= 0 and r[1] not in have:
                        pend = r[1]

    memkv(0)
    memkv(1)
    NBLK = 2 + 4 * NT + 1
    tiles = [({"kind": "h", "W": HW, "nb": 2, "gb0": 0, "rb0": 0}, lambda b: d_xh.d([[D, 128], [1, D]], off=b * 128 * D), None)]
    for t in range(2 * NT - 1):
        tiles.append(({"kind": "p", "W": 256, "nb": 2, "gb0": 2 + 2 * t, "rb0": 2 + 2 * t, "first": t == 0},
                      (lambda b, t=t: d_xo.d([[D, 128], [1, D]], off=(t * 256 + b * 128) * D)),
                      (lambda b, t=t: o_y.d([[D, 128], [1, D]], off=(t * 256 + b * 128) * D))))
    for k in range(2):
        tb_ = 2 * (2 * NT - 1) + k
        tiles.append(({"kind": "p", "W": 128, "nb": 1, "gb0": 2 + tb_, "rb0": 2 + tb_, "first": False, "last": k == 1},
                      (lambda b, tb_=tb_: d_xo.d([[D, 128], [1, D]], off=tb_ * 128 * D)),
                      (lambda b, tb_=tb_: o_y.d([[D, 128], [1, D]], off=tb_ * 128 * D))))
    lanes = [[], []]
    for i, (tl, src, dst) in enumerate(tiles):
        j = i % 2
        tl.update({"c0": 256 * j, "xb0": 2 * j, "h": j, "pair": i // 2})
        lanes[j].append((tl, src, dst))
    gens = [None, None]
    lidx = [0, 0]
    cnt = [0, 0]
    alive = [True, True]

    def start(j):
        if lidx[j] < len(lanes[j]):
            tl, src, dst = lanes[j][lidx[j]]
            lidx[j] += 1
            gens[j] = tile_gen(tl, src, dst)
        else:
            gens[j] = None
            alive[j] = False

    start(0)
    start(1)
    next(gens[0])
    next(gens[1])
    cnt[0] += 1
    cnt[1] += 1
    for f in pre_late:
        f()
    pend = None
    while alive[0] or alive[1]:
        if pend is not None and (pend in have or pend < 0 or not alive[0]):
            pend = None
        adv0 = alive[0] and (not alive[1] or pend is not None or cnt[0] - cnt[1] < LAG)
        j = 0 if adv0 else 1
        r = next(gens[j], "END")
        cnt[j] += 1
        if r == "END":
            start(j)
        elif isinstance(r, tuple):
            if r[0] == "have":
                have.add(r[1])
            elif r[0] == "need" and j == 1 and r[1] >= 0 and r[1] not in have:
                pend = r[1]

    run_pair(tile_gen({"kind": "s", "W": 128, "nb": 1, "c0": 0, "xb0": 0, "h": 0, "gb0": 0, "rb0": NBLK - 1},
                      lambda b: d_xs.d([[D, 128], [1, D]]), lambda b: o_ys.d([[D, 128], [1, D]])), None)
    P.emit(stack)
    stack.close()
    return nc


def _img_k(w):
    K, N = w.shape
    return np.ascontiguousarray(w.reshape(K // 128, 128, N).transpose(1, 0, 2))


def _rep(row):
    row = np.asarray(row, np.float32).reshape(1, -1)
    return np.repeat(row, 128, axis=0)


_NC_CACHE = {}
_HOOK = {}


def kernel(**inp):
    f32 = np.float32
    A = {k: np.asarray(v) for k, v in inp.items()}
    xp = A["x_prompt"]
    B, SEQ, _ = xp.shape
    ncores = 2 * B
    HL = SEQ // 2
    NT = HL // TW
    xsamp = A["x_sample"]
    DB = xsamp.shape[0]
    assert xsamp.shape[1] == 8 and DB == 16 * ncores and HL % TW == 0
    CO, NCONST = const_layout(NT)
    NBLK = 2 + 4 * NT + 1

    shared = {}
    shared["w_in"] = _img_k(A["ab_w_in"][0])
    shared["w_out"] = _img_k(A["ab_w_out"][0])
    shared["w_qkv"] = _img_k(A["c_w_qkv"][0])
    shared["w_o"] = _img_k(A["c_w_o"][0])
    shared["m_wq"] = np.stack([_img_k(A["m_w_q"][l]) for l in range(2)])
    shared["m_wkv"] = np.stack([_img_k(A["m_w_kv"][l]) for l in range(2)])
    shared["m_wo"] = np.stack([_img_k(A["m_w_o"][l]) for l in range(2)])
    fup = []
    for l in range(2):
        img = _img_k(A["f_w_up"][l])
        g = img[:, :, :DFF].reshape(128, 8, NFC, 128)
        u = img[:, :, DFF:].reshape(128, 8, NFC, 128)
        fup.append(np.concatenate([g, u], -1).transpose(2, 0, 1, 3))
    shared["f_up"] = np.ascontiguousarray(np.stack(fup))
    shared["f_down"] = np.stack([_img_k(A["f_w_down"][l]) for l in range(2)])

    cbase = np.zeros((128, NCONST), f32)

    def put(c, name, arr):
        arr = np.asarray(arr, f32).reshape(128, -1)
        c[:, CO[name]:CO[name] + arr.shape[1]] = arr

    for l in range(2):
        put(cbase, "g_mix%d" % l, A["ln_mix"][l].reshape(8, 128).T)
        put(cbase, "g_mem%d" % l, A["ln_mem"][l].reshape(8, 128).T)
        put(cbase, "g_ffn%d" % l, A["ln_ffn"][l].reshape(8, 128).T)
        put(cbase, "g_memkv%d" % l, A["ln_memkv"][l].reshape(8, 128).T)
        put(cbase, "mqg%d" % l, A["m_q_gain"][l].reshape(128, 1))
        put(cbase, "mkg%d" % l, _rep(A["m_k_gain"][l]))
        put(cbase, "convw%d" % l, A["f_conv_w"][l].reshape(3, NFC, 128).transpose(2, 1, 0))
        put(cbase, "convb%d" % l, A["f_conv_b"][l].reshape(NFC, 128).T)
    put(cbase, "vgain", _rep(A["ab_v_gain"][0]))
    bs = A["ab_b_s"][0]
    put(cbase, "bs_p", _rep(bs.reshape(-1)))
    put(cbase, "bs_s", _rep(np.concatenate([np.tile(bs[g, :8], 16) for g in range(4)])))
    put(cbase, "pscale", A["ab_pool_scale"][0].reshape(4, 128).T)
    put(cbase, "cqg", _rep(A["c_q_gain"][0]))
    put(cbase, "ckg", _rep(A["c_k_gain"][0]))
    put(cbase, "sinks", _rep(A["c_sinks"][0]))
    ii = np.arange(128)
    put(cbase, "m_cur", (ii[:, None] <= ii[None, :]).astype(f32))
    put(cbase, "m_prev", (ii[:, None] > ii[None, :]).astype(f32))
    bb_, tt_ = ii // 8, ii % 8
    put(cbase, "m_sn", ((bb_[:, None] == bb_[None, :]) & (tt_[:, None] <= tt_[None, :])).astype(f32))
    msc = np.zeros((128, 8), f32)
    msc[:, :] = (ii[:, None] > np.arange(8)[None, :])
    put(cbase, "m_sc", msc)
    ws = A["ab_w_s"][0]
    c2h = np.zeros((128, NC2), f32)

    def put2(name, arr):
        arr = np.asarray(arr, f32).reshape(128, -1)
        c2h[:, C2[name]:C2[name] + arr.shape[1]] = arr

    put2("wsT_p", ws.transpose(2, 0, 1))
    wss = np.zeros((16, 8, 4, 16, 8), f32)
    small = ws[:, :8, :8].transpose(2, 0, 1)
    for b in range(16):
        wss[b, :, :, b, :] = small
    put2("wsT_s", wss)
    put2("poolw", A["ab_pool_w"][0].transpose(1, 0, 2))
    put2("ident", np.eye(128, dtype=f32))
    shared["consts2"] = c2h
    half_rot = 8
    inv = (np.float32(500000.0) ** (-np.arange(half_rot, dtype=f32) / np.float32(half_rot))).astype(f32)

    in_maps = []
    for c in range(ncores):
        seq, half = c // 2, c % 2
        start = half * HL
        b0 = c * 16
        m = dict(shared)
        m["xo"] = np.ascontiguousarray(xp[seq, start:start + HL])
        m["xh"] = np.ascontiguousarray(xp[seq, start - HW:start]) if half == 1 else np.zeros((HW, D), f32)
        m["xs"] = np.ascontiguousarray(xsamp[b0:b0 + 16].reshape(128, D))
        m["xm"] = np.ascontiguousarray(A["mem_prompt"][seq])
        cc = cbase.copy()
        put(cc, "flag", np.full((128, 1), float(half), f32))
        wnd = np.array([2, 4, 8, 16], f32)
        pos16 = np.arange(16, dtype=f32)
        if half == 0:
            ic = 1.0 / np.minimum(pos16[None, :] + 1.0, wnd[:, None])
        else:
            ic = np.repeat(1.0 / wnd[:, None], 16, axis=1)
        put(cc, "invcnt", _rep(ic.astype(f32).reshape(-1)))
        put(cc, "m_pf", (ii[:, None] > ii[None, :]).astype(f32) if half == 1 else np.zeros((128, 128), f32))
        pos = np.zeros((128, NBLK), np.int64)
        for rb in range(NBLK - 1):
            pos[:, rb] = start - HW + rb * 128 + ii
        pos[:, NBLK - 1] = PAST_LEN + (ii % 8)
        ang = pos.astype(f32)[:, :, None] * inv[None, None, :]
        put(cc, "cos", np.cos(ang).astype(f32))
        put(cc, "sin", np.sin(ang).astype(f32))
        m["consts"] = cc
        m["c_pool"] = np.ascontiguousarray(A["cache_pool"][0, b0:b0 + 16].reshape(16, 15, 4, 128).transpose(3, 2, 0, 1))
        m["c_swak"] = np.ascontiguousarray(A["cache_swa_k"][0, b0:b0 + 16].reshape(16, 128, 256))
        m["c_swav"] = np.ascontiguousarray(A["cache_swa_v"][0, b0:b0 + 16].reshape(16, 128, 256))
        m["c_swakT"] = np.ascontiguousarray(A["cache_swa_k"][0, b0:b0 + 16].transpose(3, 2, 0, 1))
        m["c_memkT"] = np.ascontiguousarray(A["cache_mem_k"][:, b0:b0 + 16].transpose(0, 1, 4, 3, 2))
        m["c_memv"] = np.ascontiguousarray(A["cache_mem_v"][:, b0:b0 + 16].reshape(2, 16, 256, 512))
        m["c_conv"] = np.ascontiguousarray(A["cache_ffn_conv"][:, b0:b0 + 16].reshape(2, 16, 2, NFC, 128).transpose(0, 4, 3, 1, 2))
        in_maps.append({k: np.ascontiguousarray(v, dtype=f32) for k, v in m.items()})

    if _HOOK.get("in_maps_only"):
        return in_maps, NT
    if NT not in _NC_CACHE:
        _NC_CACHE[NT] = build(NT)
    nc = _NC_CACHE[NT]
    if _HOOK.get("results") is not None:
        R = _HOOK["results"]
    else:
        res = run_bass_kernel_spmd(nc, in_maps, core_ids=list(range(ncores)))
        R = res.results

    y_p = np.zeros((B, SEQ, D), f32); y_s = np.zeros((DB, 8, D), f32)
    pool_p = np.zeros((1, B, 15, 512), f32); pool_s = np.zeros((1, DB, 15, 512), f32)
    chunk_v = np.zeros((1, DB, 8, 512), f32)
    swak_p = np.zeros((1, B, 128, 4, 64), f32); swav_p = np.zeros((1, B, 128, 4, 64), f32)
    swak_s = np.zeros((1, DB, 128, 4, 64), f32); swav_s = np.zeros((1, DB, 128, 4, 64), f32)
    memk = np.zeros((2, B, 256, 4, 128), f32); memv = np.zeros((2, B, 256, 4, 128), f32)
    conv_p = np.zeros((2, B, 2, DFF), f32); conv_s = np.zeros((2, DB, 2, DFF), f32)
    for c in range(ncores):
        seq, half = c // 2, c % 2
        start = half * HL
        b0 = c * 16
        r = R[c]
        y_p[seq, start:start + HL] = r["y"]
        y_s[b0:b0 + 16] = r["ys"].reshape(16, 8, D)
        pool_s[0, b0:b0 + 16] = r["o_pools"].transpose(2, 3, 1, 0).reshape(16, 15, 512)
        chunk_v[0, b0:b0 + 16] = r["o_chunkv"].reshape(16, 8, 512)
        swak_s[0, b0:b0 + 16] = r["o_swaks"].reshape(16, 128, 4, 64)
        swav_s[0, b0:b0 + 16] = r["o_swavs"].reshape(16, 128, 4, 64)
        for l in range(2):
            conv_s[l, b0:b0 + 16] = r["o_convs"][l].transpose(2, 3, 1, 0).reshape(16, 2, DFF)
        if half == 1:
            pool_p[0, seq] = r["o_poolp"].transpose(2, 1, 0).reshape(15, 512)
            swak_p[0, seq] = r["o_swakp"].reshape(128, 4, 64)
            swav_p[0, seq] = r["o_swavp"].reshape(128, 4, 64)
            for l in range(2):
                memk[l, seq] = r["o_memk"][l].reshape(256, 4, 128)
                memv[l, seq] = r["o_memv"][l].reshape(256, 4, 128)
                conv_p[l, seq] = r["o_convp"][l].transpose(2, 1, 0).reshape(2, DFF)
    return (y_p, y_s, pool_p, pool_s, chunk_v, swak_p, swav_p, swak_s, swav_s, memk, memv, conv_p, conv_s)
```

```python
import numpy as np
import concourse.bass as bass
import concourse.mybir as mybir
from concourse.bass_utils import run_bass_kernel_spmd

F32 = mybir.dt.float32
BF16 = mybir.dt.bfloat16
AF = mybir.ActivationFunctionType
ALU = mybir.AluOpType
AX = mybir.AxisListType

D = 1024
KC = 8
DFF = 2816
NFC = 22
PAST_LEN = 16384
EPS = 1e-6
TW = 512
HW = 256


class Res:
    __slots__ = ("name", "lw", "rd", "dsem", "dcnt", "is_dram", "owner", "excl", "dq")

    def __init__(self, name):
        self.name = name
        self.is_dram = False
        self.owner = None
        self.excl = False
        self.dq = {}
        self.lw = None
        self.rd = set()
        self.dsem = None
        self.dcnt = 0


class V:
    __slots__ = ("ap", "res")

    def __init__(self, ap, res):
        self.ap = ap
        self.res = res if isinstance(res, (list, tuple)) else [res]


class T:
    def __init__(self, handle, shape, name, dram=False):
        self.h = handle
        self.shape = list(shape)
        self.name = name
        self.dram = dram
        self.res = Res(name)
        self.res.is_dram = dram
        self.res.owner = self
        self.sub = {}
        self.F = int(np.prod(shape[1:]))

    def d(self, dims, off=0, key=None):
        ap = bass.AP(tensor=self.h, offset=off, ap=[list(x) for x in dims])
        if not getattr(self, "track", False):
            return V(ap, [])
        return V(ap, [self.r(key)])

    def r(self, key=None):
        if key is None:
            return self.res
        if key not in self.sub:
            self.sub[key] = Res("%s/%s" % (self.name, key))
        return self.sub[key]

    def v(self, dims=None, off=0, p0=0, np_=None, key=None):
        if np_ is None:
            np_ = self.shape[0] - p0
        if dims is None:
            dims = [[1, self.F]]
        ap = bass.AP(tensor=self.h, offset=p0 * self.F + off, ap=[[self.F, np_]] + [list(d) for d in dims])
        if isinstance(key, list):
            res = [self.r(k) for k in key]
        else:
            res = self.r(key)
        return V(ap, res)


class Op:
    __slots__ = ("eng", "fn", "reads", "writes", "dma_res", "deps", "pos", "signal", "waits", "sigval", "dma_val", "dma_q")

    def __init__(self, eng, fn, reads, writes, dma_res=None):
        self.eng = eng
        self.fn = fn
        self.reads = reads
        self.writes = writes
        self.dma_res = dma_res
        self.signal = False
        self.waits = []


ENGS = ("pe", "act", "dve", "pool", "sp")


class Prog:
    def __init__(self, nc):
        self.nc = nc
        self.ops = []

    def add(self, eng, fn, reads, writes, dma_res=None):
        rr = []
        ww = []
        for v in reads:
            rr.extend(v.res)
            if eng != "pe":
                for r in v.res:
                    if r.excl:
                        ww.append(r)
        for v in writes:
            ww.extend(v.res)
        self.ops.append(Op(eng, fn, rr, ww, dma_res))

    def mm(self, out, lhsT, rhs, start=True, stop=True):
        self.add("pe", lambda e: e.matmul(out.ap, lhsT.ap, rhs.ap, start=start, stop=stop), [lhsT, rhs], [out])

    def tr(self, out, in_, ident):
        self.add("pe", lambda e: e.transpose(out.ap, in_.ap, ident.ap), [in_, ident], [out])

    def act(self, out, in_, func, bias=None, scale=None, accum=None, eng="act"):
        kw = {}
        rd = [in_]
        wr = [out]
        if bias is not None:
            kw["bias"] = bias.ap if isinstance(bias, V) else bias
            if isinstance(bias, V):
                rd.append(bias)
        if scale is not None:
            kw["scale"] = scale.ap if isinstance(scale, V) else scale
            if isinstance(scale, V):
                rd.append(scale)
        if accum is not None:
            kw["accum_out"] = accum.ap
            wr.append(accum)
        self.add(eng, lambda e: e.activation(out.ap, in_.ap, func, **kw), rd, wr)

    def tt(self, out, a, b, op, eng="dve"):
        self.add(eng, lambda e: e.tensor_tensor(out.ap, a.ap, b.ap, op), [a, b], [out])

    def ts(self, out, a, s1, s2, op0, op1=None, eng="dve"):
        rd = [a]
        x1 = s1.ap if isinstance(s1, V) else s1
        x2 = s2.ap if isinstance(s2, V) else s2
        if isinstance(s1, V):
            rd.append(s1)
        if isinstance(s2, V):
            rd.append(s2)
        if op1 is None:
            self.add(eng, lambda e: e.tensor_scalar(out.ap, a.ap, x1, None, op0), rd, [out])
        else:
            self.add(eng, lambda e: e.tensor_scalar(out.ap, a.ap, x1, x2, op0, op1), rd, [out])

    def stt(self, out, a, s, b, op0, op1, eng="dve"):
        rd = [a, b]
        xs = s.ap if isinstance(s, V) else s
        if isinstance(s, V):
            rd.append(s)
        self.add(eng, lambda e: e.scalar_tensor_tensor(out.ap, a.ap, xs, b.ap, op0, op1), rd, [out])

    def red(self, out, a, op=ALU.add, eng="dve"):
        self.add(eng, lambda e: e.tensor_reduce(out.ap, a.ap, AX.X, op), [a], [out])

    def recip(self, out, a):
        self.add("dve", lambda e: e.reciprocal(out.ap, a.ap), [a], [out])

    def copy(self, out, a, eng="dve"):
        if eng == "act":
            self.add("act", lambda e: e.activation(out.ap, a.ap, AF.Copy), [a], [out])
        else:
            self.add(eng, lambda e: e.tensor_copy(out.ap, a.ap), [a], [out])

    def memset(self, out, val, eng="dve"):
        self.add(eng, lambda e: e.memset(out.ap, val), [], [out])

    def bn(self, mv, a, st):
        self.add("dve", lambda e: e.bn_stats(st.ap, a.ap), [a], [st])
        self.add("dve", lambda e: e.bn_aggr(mv.ap, st.ap), [st], [mv])

    def dma(self, out, in_, eng="pool"):
        key = None
        for v in (out, in_):
            for r in v.res:
                key = r
                break
            if key is not None:
                break
        if key is None:
            if not hasattr(self, "d2d"):
                self.d2d = Res("d2d")
            key = self.d2d
        self.add(eng, lambda e: e.dma_start(out=out.ap, in_=in_.ap), [in_], [out], dma_res=key)

    def finalize(self):
        ops = self.ops
        pos_ctr = {e: 0 for e in ENGS}
        def expand(lst):
            out = []
            for r in lst:
                out.append(r)
                if r.owner is not None:
                    out.extend(r.owner.sub.values())
            return out

        for i, op in enumerate(ops):
            op.reads = expand(op.reads)
            op.writes = expand(op.writes)
            deps = set()
            for r in op.reads:
                if r.lw is not None:
                    deps.add(r.lw)
            for r in op.writes:
                if r.lw is not None:
                    deps.add(r.lw)
                deps |= r.rd
            for r in op.reads:
                r.rd.add(i)
            for r in op.writes:
                r.lw = i
                r.rd = set()
            deps.discard(i)
            op.deps = deps
            pos_ctr[op.eng] += 1
            op.pos = pos_ctr[op.eng]
            if op.dma_res is not None:
                q = op.dma_res.dq.setdefault(op.eng, [None, 0])
                q[1] += 16
                op.dma_val = q[1]
                op.dma_q = q
        known = {e: {e2: 0 for e2 in ENGS} for e in ENGS}
        kdma = {e: {} for e in ENGS}
        snaps = [None] * len(ops)
        for i, op in enumerate(ops):
            E = op.eng
            kn = known[E]
            need_c = {}
            need_d = {}
            for j in op.deps:
                pj = ops[j]
                if pj.dma_res is not None:
                    r = pj.dma_q
                    if kdma[E].get(id(r), 0) < pj.dma_val:
                        if need_d.get(id(r), (None, 0))[1] < pj.dma_val:
                            need_d[id(r)] = (r, pj.dma_val)
                else:
                    if pj.eng == "pe" and E == "pe":
                        continue
                    if kn[pj.eng] < pj.pos:
                        if pj.eng not in need_c or ops[need_c[pj.eng]].pos < pj.pos:
                            need_c[pj.eng] = j
            for e2, j in need_c.items():
                pj = ops[j]
                pj.signal = True
                op.waits.append(("c", j))
                sn = snaps[j]
                for e3 in ENGS:
                    if sn[e3] > kn[e3]:
                        kn[e3] = sn[e3]
                if kn[e2] < pj.pos:
                    kn[e2] = pj.pos
            for _, (r, val) in need_d.items():
                op.waits.append(("d", r, val))
                kdma[E][id(r)] = val
            sn = dict(kn)
            if op.dma_res is None and E != "sp":
                pass
            snaps[i] = sn
        sig_ctr = {e: 0 for e in ENGS}
        for op in ops:
            if op.dma_res is None and op.signal:
                sig_ctr[op.eng] += 1
                op.sigval = sig_ctr[op.eng]
        return sig_ctr

    def emit(self, stack):
        nc = self.nc
        ops = self.ops
        self.finalize()
        csem = {e: stack.enter_context(nc.semaphore("c_" + e)) for e in ENGS}
        dres = []
        for op in ops:
            if op.dma_res is not None and op.dma_q[0] is None:
                op.dma_q[0] = stack.enter_context(nc.semaphore("d%d" % len(dres)))
                dres.append(op.dma_q)
        block = stack.enter_context(nc.Block())
        by_eng = {e: [op for op in ops if op.eng == e] for e in ENGS}

        def run(e, eng_obj):
            for op in by_eng[e]:
                ws = []
                for w in op.waits:
                    if w[0] == "c":
                        pj = ops[w[1]]
                        ws.append((csem[pj.eng], pj.sigval))
                    else:
                        ws.append((w[1][0], w[2]))
                for (s, val) in ws[1:]:
                    eng_obj.wait_ge(s, val)
                ins = op.fn(eng_obj)
                if ws:
                    ins._wait_ge(ws[0][0], ws[0][1])
                if op.dma_res is not None:
                    ins.then_inc(op.dma_q[0], 16)
                elif op.signal:
                    ins.then_inc(csem[e], 1)
            if e == "sp":
                for r in dres:
                    eng_obj.wait_ge(r[0], r[1])

        @block.tensor
        def _(e):
            run("pe", e)

        @block.scalar
        def _(e):
            run("act", e)

        @block.vector
        def _(e):
            run("dve", e)

        @block.gpsimd
        def _(e):
            run("pool", e)

        @block.sync
        def _(e):
            run("sp", e)


def const_layout(NT):
    NBLK = 2 + 4 * NT + 1
    items = []
    for l in range(2):
        items += [("g_mix%d" % l, 8), ("g_mem%d" % l, 8), ("g_ffn%d" % l, 8), ("g_memkv%d" % l, 8),
                  ("mqg%d" % l, 1), ("mkg%d" % l, 128), ("convw%d" % l, 66), ("convb%d" % l, 22)]
    items += [("vgain", 512), ("bs_p", 512), ("bs_s", 512), ("pscale", 4), ("invcnt", 64), ("flag", 1),
              ("cqg", 64), ("ckg", 64), ("sinks", 16), ("cos", NBLK * 8), ("sin", NBLK * 8),
              ("m_cur", 128), ("m_prev", 128), ("m_pf", 128), ("m_sn", 128), ("m_sc", 8)]
    off = {}
    o = 0
    for k, n in items:
        off[k] = o
        o += n
    return off, o


C2 = {"wsT_p": 0, "wsT_s": 512, "poolw": 1024, "ident": 1536}
NC2 = 1664


def build(NT):
    from contextlib import ExitStack
    nc = bass.Bass("TRN2", target_bir_lowering=False)
    stack = ExitStack()
    P = Prog(nc)
    CO, NCONST = const_layout(NT)
    NTOK = NT * TW

    def dr(name, shape, kind="ExternalInput", dt=F32):
        h = nc.dram_tensor(name, list(shape), dt, kind=kind)
        return T(h, shape, name, dram=True)

    def sb(name, shape, dt=F32):
        h = stack.enter_context(nc.sbuf_tensor(name, list(shape), dt))
        return T(h, shape, name)

    def psum(name, shape, dt=F32):
        h = stack.enter_context(nc.psum_tensor(name, list(shape), dt))
        t = T(h, shape, name)
        t.res.excl = True
        return t

    d_xh = dr("xh", [HW, D]); d_xo = dr("xo", [NTOK, D]); d_xs = dr("xs", [128, D]); d_xm = dr("xm", [256, D])
    d_c = dr("consts", [128, NCONST])
    d_c2 = dr("consts2", [128, NC2])
    d_win = dr("w_in", [128, 8, 1536]); d_wout = dr("w_out", [128, 8, 1024]); d_wqkv = dr("w_qkv", [128, 8, 1536])
    d_wo = dr("w_o", [128, 8, 1024])
    d_mwq = dr("m_wq", [2, 128, 8, 512]); d_mwkv = dr("m_wkv", [2, 128, 8, 1024]); d_mwo = dr("m_wo", [2, 128, 4, 1024])
    d_fup = dr("f_up", [2, NFC, 128, 8, 256]); d_fdn = dr("f_down", [2, 128, NFC, 1024])
    twins = {}

    def twin(t):
        h = nc.dram_tensor(t.name + "_b", list(t.shape), BF16, kind="Internal")
        tw = T(h, t.shape, t.name + "_b", dram=True)
        tw.track = True
        twins[t.name] = (t, tw)
        return tw

    b_mwkv = twin(d_mwkv); b_win = twin(d_win); b_wout = twin(d_wout); b_mwq = twin(d_mwq); b_mwo = twin(d_mwo)
    b_fup = twin(d_fup); b_fdn = twin(d_fdn); b_wqkv = twin(d_wqkv); b_wo = twin(d_wo)
    d_cpool = dr("c_pool", [128, 4, 16, 15])
    d_cswak = dr("c_swak", [16, 128, 256]); d_cswav = dr("c_swav", [16, 128, 256]); d_cswakT = dr("c_swakT", [64, 4, 16, 128])
    d_cmemkT = dr("c_memkT", [2, 16, 128, 4, 256]); d_cmemv = dr("c_memv", [2, 16, 256, 512])
    d_cconv = dr("c_conv", [2, 128, NFC, 16, 2])
    O = "ExternalOutput"
    o_y = dr("y", [NTOK, D], O); o_ys = dr("ys", [128, D], O)
    o_poolp = dr("o_poolp", [128, 4, 15], O); o_pools = dr("o_pools", [128, 4, 16, 15], O); o_chunkv = dr("o_chunkv", [128, 512], O)
    o_swakp = dr("o_swakp", [128, 256], O); o_swavp = dr("o_swavp", [128, 256], O)
    o_swaks = dr("o_swaks", [16, 128, 256], O); o_swavs = dr("o_swavs", [16, 128, 256], O)
    o_memk = dr("o_memk", [2, 256, 512], O); o_memv = dr("o_memv", [2, 256, 512], O)
    o_convp = dr("o_convp", [2, 128, NFC, 2], O); o_convs = dr("o_convs", [2, 128, NFC, 16, 2], O)

    C = sb("C", [128, NCONST])
    X = sb("X", [128, 4, D])
    HT = sb("HT", [128, 8, 512], BF16)
    HN = [sb("HN%d" % i, [128, D], BF16) for i in range(2)]
    JUNK = sb("JUNK", [128, D], BF16)
    STAT = sb("STAT", [128, 256])
    IDB = sb("IDB", [128, 128], BF16); ONES = sb("ONES", [128, 128], BF16)
    MASKB = sb("MASKB", [128, 4, 128], BF16); MSCB = sb("MSCB", [128, 8], BF16)
    WSTB = sb("WSTB", [128, 8, 128], BF16); POOLWB = sb("POOLWB", [128, 4, 128], BF16)
    ESINK = sb("ESINK", [128, 16]); EPSB = sb("EPSB", [128, 1])
    MKT = [sb("MKT%d" % l, [128, 4, 256], BF16) for l in range(2)]
    MVV = [sb("MVV%d" % l, [128, 2, 512], BF16) for l in range(2)]
    CT = [sb("CT%d" % l, [128, NFC, 2]) for l in range(2)]
    CS = sb("CS", [128, 2, NFC, 16, 2])
    U = sb("U", [128, 4, 512])
    VG = sb("VG", [128, 528]); VO = sb("VO", [128, 528]); VB = sb("VB", [128, 512], BF16)
    SG = sb("SG", [128, 512])
    PEXT = sb("PEXT", [128, 4, 271]); PEXS = sb("PEXS", [128, 4, 16, 23])
    PT0 = VG; PT1 = VO
    POOLED = sb("POOLED", [128, 4, 512], BF16)
    ABT = sb("ABT", [128, 16, 512], BF16)
    QT = sb("QT", [128, 4, 512], BF16)
    KTR = sb("KTR", [64, 4, 4, 128], BF16); VVR = sb("VVR", [128, 4, 512], BF16)
    KTN = sb("KTN", [64, 4, 128], BF16); VNB = sb("VNB", [128, 256], BF16)
    PTB = [sb("PTB%d" % i, [128, 2, 512], BF16) for i in range(2)]
    RD = sb("RD", [128, 512])
    OT = sb("OT", [128, 4, 512], BF16)
    SQ = sb("SQ", [128, 512]); QF = sb("QF", [128, 512]); QB = sb("QB", [128, 512], BF16)
    RT = sb("RT", [128, 4, 64])
    HF = sb("HF", [128, 11, 512], BF16)
    GE = sb("GE", [128, 4, 260])
    ACC = sb("ACC", [128, 4, 256])
    GL = sb("GL", [128, 4, 256], BF16)
    NS = 6
    RING = [sb("RING%d" % i, [128, 4096], BF16) for i in range(NS)]
    PS = [psum("PS%d" % i, [128, 512]) for i in range(6)]
    TB = [psum("TB%d" % i, [128, 1024], BF16) for i in range(2)]
    st = {"ps": 0, "tb": 0, "ring": 0, "stat": 0, "hn": 0, "ge": 0}

    def nextps():
        st["ps"] = (st["ps"] + 1) % 6
        return PS[st["ps"]]

    def nexttb():
        st["tb"] = (st["tb"] + 1) % 2
        return TB[st["tb"]]

    def statcol(n=1):
        c = st["stat"]
        if (c % 16) + n > 16:
            c = (c // 16 + 1) * 16
        if c + n > 256:
            c = 0
        st["stat"] = c + n
        return c

    def sv(c, n=1, dims=None):
        return STAT.v(dims if dims is not None else [[1, n]], off=c, key=c // 16)

    def wload(src, np_=128, eng="sp"):
        st["ring"] = (st["ring"] + 1) % NS
        slot = RING[st["ring"]]
        n = 1
        dims = []
        shp = src.ap.shape[1:]
        for s_ in shp:
            n *= s_
        stride = n
        for s_ in shp:
            stride //= s_
            dims.append([stride, s_])
        P.dma(slot.v(dims, np_=np_), src, eng=eng)
        return slot

    def cv(name, dims, add=0, np_=128):
        return C.v(dims, off=CO[name] + add, np_=np_)

    P.dma(C.v(), d_c.d([[NCONST, 128], [1, NCONST]]))
    P.dma(U.v([[1, NC2]]), d_c2.d([[NC2, 128], [1, NC2]]))
    c2 = lambda name, dims: U.v(dims, off=C2[name])
    PCHAIN = Res("pchain")

    def prepass(name, off, n, key):
        src, dst = twins[name]
        o = dst.d([[n, 1], [1, n]], off=off, key=key)
        o.res.append(PCHAIN)
        P.dma(o, src.d([[n, 1], [1, n]], off=off), eng="pool")

    LW = {"m_wkv": 128 * 8 * 1024, "m_wq": 128 * 8 * 512, "m_wo": 128 * 4 * 1024, "f_up": NFC * 128 * 2048, "f_down": 128 * NFC * 1024}
    pre_early = [lambda l=l: prepass("m_wkv", l * LW["m_wkv"], LW["m_wkv"], l) for l in range(2)]
    pre_late = [lambda: prepass("w_in", 0, 128 * 8 * 1536, 0), lambda: prepass("w_out", 0, 128 * 8 * 1024, 0)]
    for l in range(2):
        if l == 1:
            pre_late.append(lambda: prepass("w_qkv", 0, 128 * 8 * 1536, 0))
            pre_late.append(lambda: prepass("w_o", 0, 128 * 8 * 1024, 0))
        pre_late.append(lambda l=l: prepass("m_wq", l * LW["m_wq"], LW["m_wq"], l))
        pre_late.append(lambda l=l: prepass("m_wo", l * LW["m_wo"], LW["m_wo"], l))
        pre_late.append(lambda l=l: prepass("f_up", l * LW["f_up"], 11 * 128 * 2048, (l, 0)))
        pre_late.append(lambda l=l: prepass("f_down", l * LW["f_down"], LW["f_down"], l))
        pre_late.append(lambda l=l: prepass("f_up", l * LW["f_up"] + 11 * 128 * 2048, 11 * 128 * 2048, (l, 1)))
    P.copy(IDB.v(), c2("ident", [[1, 128]]))
    P.memset(ONES.v(), 1.0)
    P.memset(EPSB.v(), EPS)
    for i, nm in enumerate(["m_cur", "m_prev", "m_pf", "m_sn"]):
        P.copy(MASKB.v([[1, 128]], off=i * 128), cv(nm, [[1, 128]]))
    P.copy(MSCB.v(), cv("m_sc", [[1, 8]]))
    P.tt(WSTB.v([[128, 4], [1, 128]]), c2("wsT_p", [[128, 4], [1, 128]]), cv("m_cur", [[0, 4], [1, 128]]), ALU.mult)
    P.tt(WSTB.v([[128, 4], [1, 128]], off=512), c2("wsT_s", [[128, 4], [1, 128]]), cv("m_sn", [[0, 4], [1, 128]]), ALU.mult)
    P.copy(POOLWB.v(), c2("poolw", [[1, 512]]))
    P.act(ESINK.v(), cv("sinks", [[1, 16]]), AF.Exp)
    P.memset(PEXT.v(), 0.0)
    P.memset(KTR.v(), 0.0)
    P.memset(VVR.v(), 0.0)
    for l in range(2):
        P.memset(CT[l].v(), 0.0)
    P.dma(CS.v([[NFC * 32, 2], [1, NFC * 32]]), d_cconv.d([[NFC * 32, 128], [128 * NFC * 32, 2], [1, NFC * 32]]))
    P.dma(X.v([[1, 960]]), d_cpool.d([[960, 128], [1, 960]]))
    P.copy(PEXS.v([[23, 64], [1, 15]]), X.v([[15, 64], [1, 15]]))
    P.dma(o_swaks.d([[128 * 256, 16], [1, 120 * 256]]), d_cswak.d([[128 * 256, 16], [1, 120 * 256]], off=8 * 256))
    P.dma(o_swavs.d([[128 * 256, 16], [1, 120 * 256]]), d_cswav.d([[128 * 256, 16], [1, 120 * 256]], off=8 * 256))

    def xk(nb):
        return list(range(nb))

    wcache = {}
    st["ringctr"] = 0

    def wget(pair, key, fn):
        key = (pair, key)
        ent = wcache.get(key)
        if ent is not None and st["ringctr"] - ent[1] < NS - 1:
            return ent[0]
        slot = fn()
        if not isinstance(slot, list):
            wcache[key] = (slot, st["ringctr"])
        else:
            wcache[key] = (slot, st["ringctr"] - len(slot) + 1)
        return slot

    def wl(src, np_=128, eng="sp"):
        st["ringctr"] += 1
        return wload(src, np_=np_, eng=eng)

    def rsqrt_cols(c_in, c_out, n, scale):
        P.act(sv(c_out, n), sv(c_in, n), AF.Ln, bias=EPSB.v(), scale=scale)
        P.act(sv(c_out, n), sv(c_out, n), AF.Exp, scale=-0.5)

    def norm(tl, gname):
        nb, c0, xb0 = tl["nb"], tl["c0"], tl["xb0"]
        c = statcol(2 * nb)
        for b in range(nb):
            xb = X.v([[1, D]], off=(xb0 + b) * D, key=xb0 + b)
            P.act(JUNK.v(), xb, AF.Square, accum=sv(c + b))
        yield
        rsqrt_cols(c, c + nb, nb, 1.0 / D)
        for b in range(nb):
            xb = X.v([[1, D]], off=(xb0 + b) * D, key=xb0 + b)
            st["hn"] ^= 1
            hn = HN[st["hn"]]
            P.act(hn.v(), xb, AF.Identity, scale=sv(c + nb + b))
            tb = nexttb()
            for kc in range(8):
                P.tr(tb.v([[1, 128]], off=kc * 128), hn.v([[1, 128]], off=kc * 128), IDB.v())
            P.tt(HT.v([[512, 8], [1, 128]], off=c0 + b * 128, key=xb0 + b), tb.v([[128, 8], [1, 128]]),
                 cv(gname, [[1, 8], [0, 128]]), ALU.mult)
        yield

    def resid(tl, b, half, ps):
        xv = X.v([[1, 512]], off=(tl["xb0"] + b) * D + half * 512, key=tl["xb0"] + b)
        P.tt(xv, xv, ps.v(), ALU.add)

    def head_rstd(ps, nh, hd):
        P.act(SQ.v([[1, nh * hd]]), ps.v([[1, nh * hd]]), AF.Square)
        c = statcol(2 * nh)
        P.red(sv(c, nh), SQ.v([[hd, nh], [1, hd]]))
        rsqrt_cols(c, c + nh, nh, 1.0 / hd)
        return c + nh

    def htk(tl):
        return [tl["xb0"] + b for b in range(tl["nb"])]

    def memkv(l):
        tl = {"nb": 2, "c0": 0, "xb0": 0, "W": 256}
        if l == 0:
            for b in range(2):
                P.dma(X.v([[1, D]], off=b * D, key=b), d_xm.d([[D, 128], [1, D]], off=b * 128 * D))
            for f in pre_early:
                f()
        for _ in norm(tl, "g_memkv%d" % l):
            pass
        wk = wl(b_mwkv.d([[8 * 1024, 128], [1024, 8], [1, 512]], off=l * 128 * 8 * 1024, key=l))
        wv = wl(b_mwkv.d([[8 * 1024, 128], [1024, 8], [1, 512]], off=l * 128 * 8 * 1024 + 512, key=l))
        for b in range(2):
            ps = nextps()
            for kc in range(8):
                P.mm(ps.v(), HT.v([[1, 128]], off=kc * 512 + b * 128, key=b), wk.v([[1, 512]], off=kc * 512), kc == 0, kc == 7)
            rc = head_rstd(ps, 4, 128)
            P.tt(QF.v([[128, 4], [1, 128]]), ps.v([[128, 4], [1, 128]]), sv(rc, dims=[[1, 4], [0, 128]]), ALU.mult)
            P.tt(QF.v([[128, 4], [1, 128]]), QF.v([[128, 4], [1, 128]]), cv("mkg%d" % l, [[0, 4], [1, 128]]), ALU.mult)
            P.dma(o_memk.d([[512, 128], [1, 512]], off=(l * 256 + b * 128) * 512), QF.v())
            P.copy(QB.v(), QF.v())
            tb = nexttb()
            for h in range(4):
                P.tr(tb.v([[1, 128]], off=h * 128), QB.v([[1, 128]], off=h * 128), IDB.v())
            P.copy(MKT[l].v([[256, 4], [1, 128]], off=b * 128), tb.v([[128, 4], [1, 128]]), eng="act")
            ps = nextps()
            for kc in range(8):
                P.mm(ps.v(), HT.v([[1, 128]], off=kc * 512 + b * 128, key=b), wv.v([[1, 512]], off=kc * 512), kc == 0, kc == 7)
            P.copy(VO.v([[1, 512]]), ps.v(), eng="act")
            P.dma(o_memv.d([[512, 128], [1, 512]], off=(l * 256 + b * 128) * 512), VO.v([[1, 512]]))
            P.copy(MVV[l].v([[1, 512]], off=b * 512), VO.v([[1, 512]]))

    def mix_ab(tl):
        W, nb, kind, c0, xb0, hh = tl["W"], tl["nb"], tl["kind"], tl["c0"], tl["xb0"], tl["h"]
        sample = kind == "s"
        allb = htk(tl)
        yield from norm(tl, "g_mix0")
        wu = wget(tl.get('pair', -1), "w_in_u", lambda: wl(b_win.d([[8 * 1536, 128], [1536, 8], [1, 512]], off=0, key=0)))
        for g in range(4):
            ps = nextps()
            for kc in range(8):
                P.mm(ps.v([[1, W]]), wu.v([[1, 128]], off=kc * 512 + g * 128), HT.v([[1, W]], off=kc * 512 + c0, key=allb), kc == 0, kc == 7)
            P.act(U.v([[1, W]], off=g * 512 + c0, key=(g, hh)), ps.v([[1, W]]), AF.Gelu)
        yield
        wv = wget(tl.get('pair', -1), "w_in_v", lambda: wl(b_win.d([[8 * 1536, 128], [1536, 8], [1, 512]], off=512, key=0)))
        for b in range(nb):
            ps = nextps()
            for kc in range(8):
                P.mm(ps.v(), HT.v([[1, 128]], off=kc * 512 + c0 + b * 128, key=xb0 + b), wv.v([[1, 512]], off=kc * 512), kc == 0, kc == 7)
            P.act(VG.v([[1, 512]]), ps.v(), AF.Gelu)
            cst = statcol(9)
            P.bn(sv(cst, 2), VG.v([[1, 512]]), sv(cst + 2, 6))
            P.act(sv(cst + 8), sv(cst + 1), AF.Ln, bias=EPSB.v(), scale=1.0)
            P.act(sv(cst + 8), sv(cst + 8), AF.Exp, scale=-0.5)
            P.ts(VG.v([[1, 512]]), VG.v([[1, 512]]), sv(cst), sv(cst + 8), ALU.subtract, ALU.mult)
            P.tt(VO.v([[1, 512]]), VG.v([[1, 512]]), cv("vgain", [[1, 512]]), ALU.mult)
            VBx = [VB, QB][hh]
            P.copy(VBx.v(), VO.v([[1, 512]]))
            if sample:
                P.dma(o_chunkv.d([[512, 128], [1, 512]]), VO.v([[1, 512]]))
            yield
            psg = nextps()
            for g in range(4):
                P.mm(psg.v([[1, 128]], off=g * 128), VBx.v([[1, 128]], off=g * 128),
                     WSTB.v([[1, 128]], off=((4 if sample else 0) + g) * 128))
            P.tt(SG.v(), psg.v(), cv("bs_s" if sample else "bs_p", [[1, 512]]), ALU.add)
            P.tt(ABT.v([[512, 4], [1, 128]], off=c0 + b * 128, key="a%d" % (xb0 + b)), SG.v([[128, 4], [1, 128]]),
                 U.v([[512, 4], [1, 128]], off=c0 + b * 128, key=[(g, hh) for g in range(4)]), ALU.mult)
            yield
        wp = wget(tl.get('pair', -1), "w_in_p", lambda: wl(b_win.d([[8 * 1536, 128], [1536, 8], [1, 512]], off=1024, key=0)))
        def pool_b(g):
            ps2 = nextps()
            P.mm(ps2.v([[1, W]]), POOLWB.v([[1, 128]], off=g * 128), POOLED.v([[1, W]], off=g * 512 + c0, key=(g, hh)))
            P.act(ABT.v([[1, W]], off=(4 + g) * 512 + c0, key=("p", g, hh)), ps2.v([[1, W]]), AF.Identity, scale=cv("pscale", [[1, 1]], add=g))

        if sample:
            nseg, L, Tt = 16, 23, 8
            PE_, pstride = PEXS, 16 * 23
        else:
            nseg, L, Tt = 1, 15 + W, W
            PE_, pstride = PEXT, 271
        for g in range(4):
            ps = nextps()
            for kc in range(8):
                P.mm(ps.v([[1, W]]), wp.v([[1, 128]], off=kc * 512 + g * 128), HT.v([[1, W]], off=kc * 512 + c0, key=allb), kc == 0, kc == 7)
            P.copy(PE_.v([[L, nseg], [1, Tt]], off=g * pstride + 15, key=g), ps.v([[Tt, nseg], [1, Tt]]), eng="act")
            src = (PE_, g * pstride, g)
            bufs = [PT0, PT1]
            for k in range(g + 1):
                sh = 1 << k
                lo = (1 << (k + 1)) - 1
                dst = bufs[k % 2]
                s_t, s_off, s_key = src
                P.tt(dst.v([[L, nseg], [1, L - lo]], off=lo),
                     s_t.v([[L, nseg], [1, L - lo]], off=s_off + lo, key=s_key),
                     s_t.v([[L, nseg], [1, L - lo]], off=s_off + lo - sh, key=s_key), ALU.add)
                src = (dst, 0, None)
            s_t, s_off, s_key = src
            w = 1 << (g + 1)
            P.stt(POOLED.v([[Tt, nseg], [1, Tt]], off=g * 512 + c0, key=(g, hh)),
                  s_t.v([[L, nseg], [1, Tt]], off=s_off + 15, key=s_key), 1.0 / w,
                  PE_.v([[L, nseg], [1, Tt]], off=g * pstride + 15, key=g), ALU.mult, ALU.subtract)
            if tl.get("first"):
                P.tt(SG.v([[1, 16]]), s_t.v([[1, 16]], off=s_off + 15, key=s_key), cv("invcnt", [[1, 16]], add=g * 16), ALU.mult)
                P.tt(POOLED.v([[1, 16]], off=g * 512 + c0, key=(g, hh)), SG.v([[1, 16]]), PE_.v([[1, 16]], off=g * pstride + 15, key=g), ALU.subtract)
            if not sample:
                P.copy(PEXT.v([[1, 15]], off=g * 271, key=g), PEXT.v([[1, 15]], off=g * 271 + W, key=g))
                if kind == "h":
                    P.ts(PEXT.v([[1, 15]], off=g * 271, key=g), PEXT.v([[1, 15]], off=g * 271, key=g), cv("flag", [[1, 1]]), None, ALU.mult)
            yield
            if g > 0:
                pool_b(g - 1)
                yield
        pool_b(3)
        yield
        if sample:
            P.copy(U.v([[15, 64], [1, 15]]), PEXS.v([[23, 64], [1, 15]], off=8, key=[0, 1, 2, 3]))
            P.dma(o_pools.d([[960, 128], [1, 960]]), U.v([[1, 960]]))
        elif tl.get("last"):
            P.dma(o_poolp.d([[60, 128], [15, 4], [1, 15]]), PEXT.v([[271, 4], [1, 15]], key=[0, 1, 2, 3]))
        abk = ["a%d" % (xb0 + b) for b in range(nb)] + [("p", g, hh) for g in range(4)]
        for half in range(2):
            wo = wget(tl.get('pair', -1), ("w_out", half), lambda: wl(b_wout.d([[8 * 1024, 128], [1024, 8], [1, 512]], off=half * 512, key=0)))
            for b in range(nb):
                ps = nextps()
                for kc in range(8):
                    P.mm(ps.v(), ABT.v([[1, 128]], off=kc * 512 + c0 + b * 128, key=abk), wo.v([[1, 512]], off=kc * 512), kc == 0, kc == 7)
                resid(tl, b, half, ps)
            yield

    def mem_attn(tl, l):
        W, nb, kind, c0, xb0, hh = tl["W"], tl["nb"], tl["kind"], tl["c0"], tl["xb0"], tl["h"]
        sample = kind == "s"
        allb = htk(tl)
        yield from norm(tl, "g_mem%d" % l)
        wq = wget(tl.get('pair', -1), ("m_wq", l), lambda: wl(b_mwq.d([[8 * 512, 128], [512, 8], [1, 512]], off=l * 128 * 8 * 512, key=l)))
        for b in range(nb):
            ps = nextps()
            for kc in range(8):
                P.mm(ps.v(), HT.v([[1, 128]], off=kc * 512 + c0 + b * 128, key=xb0 + b), wq.v([[1, 512]], off=kc * 512), kc == 0, kc == 7)
            rc = head_rstd(ps, 4, 128)
            QBx = [QB, VB][hh]
            P.tt(QBx.v([[128, 4], [1, 128]]), ps.v([[128, 4], [1, 128]]), sv(rc, dims=[[1, 4], [0, 128]]), ALU.mult)
            yield
            tb = nexttb()
            for h in range(4):
                P.tr(tb.v([[1, 128]], off=h * 128), QBx.v([[1, 128]], off=h * 128), IDB.v())
            P.act(QT.v([[512, 4], [1, 128]], off=c0 + b * 128, key=xb0 + b), tb.v([[128, 4], [1, 128]]), AF.Identity, scale=cv("mqg%d" % l, [[1, 1]]))
            yield
        sc = 128.0 ** -0.5
        if not sample:
            for h in range(4):
                pt = PTB[h % 2]
                for mc in range(2):
                    ps = nextps()
                    P.mm(ps.v([[1, W]]), MKT[l].v([[1, 128]], off=h * 256 + mc * 128), QT.v([[1, W]], off=h * 512 + c0, key=allb))
                    P.act(pt.v([[1, W]], off=mc * 512 + c0, key=(mc, hh)), ps.v([[1, W]]), AF.Exp, scale=sc)
                yield
                pso = nextps(); psd = nextps()
                for mc in range(2):
                    P.mm(pso.v([[1, W]]), MVV[l].v([[1, 128]], off=mc * 512 + h * 128), pt.v([[1, W]], off=mc * 512 + c0, key=(mc, hh)), mc == 0, mc == 1)
                for mc in range(2):
                    P.mm(psd.v([[1, W]]), ONES.v(), pt.v([[1, W]], off=mc * 512 + c0, key=(mc, hh)), mc == 0, mc == 1)
                P.act(RD.v([[1, W]], off=c0, key=hh), psd.v([[1, W]]), AF.Ln)
                P.act(RD.v([[1, W]], off=c0, key=hh), RD.v([[1, W]], off=c0, key=hh), AF.Exp, scale=-1.0)
                P.tt(OT.v([[1, W]], off=h * 512 + c0, key=(h, hh)), pso.v([[1, W]]), RD.v([[1, W]], off=c0, key=hh), ALU.mult)
                yield
        else:
            pss = [nextps(), nextps()]
            for bb in range(16):
                kt = wl(d_cmemkT.d([[1024, 128], [1, 1024]], off=(l * 16 + bb) * 128 * 1024), eng="pool")
                for h in range(4):
                    for mc in range(2):
                        P.mm(pss[h // 2].v([[1, 8]], off=((h % 2) * 2 + mc) * 128 + bb * 8),
                             kt.v([[1, 128]], off=h * 256 + mc * 128), QT.v([[1, 8]], off=h * 512 + bb * 8, key=0))
            for i in range(2):
                P.act(PTB[i].v([[1, 512]]), pss[i].v(), AF.Exp, scale=sc)
            pso = nextps(); psd = nextps()
            for bb in range(16):
                vv = wl(d_cmemv.d([[512, 128], [128 * 512, 2], [1, 512]], off=(l * 16 + bb) * 256 * 512), eng="pool")
                for h in range(4):
                    for mc in range(2):
                        P.mm(pso.v([[1, 8]], off=h * 128 + bb * 8), vv.v([[1, 128]], off=mc * 512 + h * 128),
                             PTB[h // 2].v([[1, 8]], off=((h % 2) * 2 + mc) * 128 + bb * 8), mc == 0, mc == 1)
            for h in range(4):
                for mc in range(2):
                    P.mm(psd.v([[1, 128]], off=h * 128), ONES.v(), PTB[h // 2].v([[1, 128]], off=((h % 2) * 2 + mc) * 128), mc == 0, mc == 1)
            P.act(RD.v(), psd.v(), AF.Ln)
            P.act(RD.v(), RD.v(), AF.Exp, scale=-1.0)
            P.tt(OT.v([[512, 4], [1, 128]], key=[(h, hh) for h in range(4)]), pso.v([[128, 4], [1, 128]]), RD.v([[128, 4], [1, 128]]), ALU.mult)
        for half in range(2):
            wo = wget(tl.get('pair', -1), ("m_wo", l, half), lambda: wl(b_mwo.d([[4 * 1024, 128], [1024, 4], [1, 512]], off=l * 128 * 4 * 1024 + half * 512, key=l)))
            for b in range(nb):
                ps = nextps()
                for h in range(4):
                    P.mm(ps.v(), OT.v([[1, 128]], off=h * 512 + c0 + b * 128, key=(h, hh)), wo.v([[1, 512]], off=h * 512), h == 0, h == 3)
                resid(tl, b, half, ps)
            yield

    def ffn(tl, l):
        W, nb, kind, c0, xb0, hh = tl["W"], tl["nb"], tl["kind"], tl["c0"], tl["xb0"], tl["h"]
        sample = kind == "s"
        allb = htk(tl)
        yield from norm(tl, "g_ffn%d" % l)
        if sample:
            nseg, L, Tt = 16, 10, 8
        else:
            nseg, L, Tt = 1, 2 + W, W
        def stage1(c, ci):
            w = wget(tl.get('pair', -1), ("fup", l, c), lambda: wl(b_fup.d([[2048, 128], [1, 2048]], off=(l * NFC + c) * 128 * 2048, key=(l, c // 11))))
            psg = nextps()
            for kc in range(8):
                P.mm(psg.v([[1, W]]), w.v([[1, 128]], off=kc * 256), HT.v([[1, W]], off=kc * 512 + c0, key=allb), kc == 0, kc == 7)
            idx = hh * 2 + (c % 2)
            geo, aco = idx * 260, idx * 256
            gev = lambda dims, off=0: GE.v(dims, off=geo + off, key=idx)
            if sample:
                P.copy(gev([[L, 16], [1, 2]]), CS.v([[2, 16], [1, 2]], off=(l * NFC + c) * 32))
            else:
                P.copy(gev([[1, 2]]), CT[l].v([[1, 2]], off=c * 2, key=c))
            cw = lambda j: cv("convw%d" % l, [[1, 1]], add=c * 3 + j)
            av = ACC.v([[Tt, nseg], [1, Tt]], off=aco, key=idx)
            P.act(av, psg.v([[Tt, nseg], [1, Tt]]), AF.Identity, bias=cv("convb%d" % l, [[1, 1]], add=c), scale=cw(2))
            P.copy(gev([[L, nseg], [1, Tt]], 2), psg.v([[Tt, nseg], [1, Tt]]), eng="act")
            if sample:
                P.copy(U.v([[2, 16], [1, 2]], off=c * 32), gev([[L, 16], [1, 2]], 8))
            else:
                P.copy(CT[l].v([[1, 2]], off=c * 2, key=c), gev([[1, 2]], W))
                if kind == "h":
                    P.ts(CT[l].v([[1, 2]], off=c * 2, key=c), CT[l].v([[1, 2]], off=c * 2, key=c), cv("flag", [[1, 1]]), None, ALU.mult)
            cw = lambda j: cv("convw%d" % l, [[1, 1]], add=c * 3 + j)
            av = ACC.v([[Tt, nseg], [1, Tt]], off=aco, key=idx)
            P.stt(av, gev([[L, nseg], [1, Tt]], 1), cw(1), av, ALU.mult, ALU.add)
            P.stt(av, gev([[L, nseg], [1, Tt]], 0), cw(0), av, ALU.mult, ALU.add)
            return dict(c=c, ci=ci, w=w, idx=idx, aco=aco)

        def stage2(stt_):
            c, ci, idx, aco = stt_["c"], stt_["ci"], stt_["idx"], stt_["aco"]
            w = wget(tl.get('pair', -1), ("fup", l, c), lambda: wl(b_fup.d([[2048, 128], [1, 2048]], off=(l * NFC + c) * 128 * 2048, key=(l, c // 11))))
            psu = nextps()
            for kc in range(8):
                P.mm(psu.v([[1, W]]), w.v([[1, 128]], off=kc * 256 + 128), HT.v([[1, W]], off=kc * 512 + c0, key=allb), kc == 0, kc == 7)
            P.act(GL.v([[1, W]], off=aco, key=idx), ACC.v([[1, W]], off=aco, key=idx), AF.Gelu)
            P.tt(HF.v([[1, W]], off=ci * 512 + c0, key=(ci, hh)), GL.v([[1, W]], off=aco, key=idx), psu.v([[1, W]]), ALU.mult)

        for hf in range(2):
            prev_st = None
            for ci in range(11):
                cur_st = stage1(hf * 11 + ci, ci)
                yield
                if prev_st is not None:
                    stage2(prev_st)
                    yield
                prev_st = cur_st
            stage2(prev_st)
            yield
            wd = wget(tl.get('pair', -1), ("fdn", l, hf), lambda: [wl(b_fdn.d([[NFC * 1024, 128], [1, n * 1024]], off=l * 128 * NFC * 1024 + (hf * 11 + c0_) * 1024, key=l))
                                                for (c0_, n) in ((0, 4), (4, 4), (8, 3))])
            for b in range(nb):
                for half in range(2):
                    ps = nextps()
                    for ci in range(11):
                        P.mm(ps.v(), HF.v([[1, 128]], off=ci * 512 + c0 + b * 128, key=(ci, hh)),
                             wd[ci // 4].v([[1, 512]], off=(ci % 4) * 1024 + half * 512), ci == 0, ci == 10)
                    resid(tl, b, half, ps)
            yield
        if sample:
            P.dma(o_convs.d([[NFC * 32, 128], [1, NFC * 32]], off=l * 128 * NFC * 32), U.v([[1, NFC * 32]]))
        elif tl.get("last"):
            P.dma(o_convp.d([[NFC * 2, 128], [1, NFC * 2]], off=l * 128 * NFC * 2), CT[l].v(key=list(range(NFC))))

    OTS = ABT
    QTS = [QT, POOLED]

    def swa(tl):
        W, nb, kind, c0, xb0, hh = tl["W"], tl["nb"], tl["kind"], tl["c0"], tl["xb0"], tl["h"]
        sample = kind == "s"
        QTx = QTS[hh]
        yield from norm(tl, "g_mix1")
        wq = [wget(tl.get('pair', -1), ("w_qkv", p), lambda: wl(b_wqkv.d([[8 * 1536, 128], [1536, 8], [1, 512]], off=p * 512, key=0))) for p in range(3)]
        if sample:
            kc_ = [wl(d_cswakT.d([[4 * 16 * 128, 64], [1, 2 * 16 * 128]], off=i * 2 * 16 * 128), np_=64, eng="pool") for i in range(2)]
            vc_ = wl(d_cswav.d([[256, 128], [128 * 256, 16], [1, 256]]), eng="pool")
        for b in range(nb):
            gb = tl["gb0"] + b
            slot = gb % 4
            xb = xb0 + b
            for part in range(3):
                ps = nextps()
                for kc in range(8):
                    P.mm(ps.v(), HT.v([[1, 128]], off=kc * 512 + c0 + b * 128, key=xb), wq[part].v([[1, 512]], off=kc * 512), kc == 0, kc == 7)
                nh = 8 if part < 2 else 4
                rc = head_rstd(ps, nh, 64)
                qf = lambda dims, off=0: QF.v(dims, off=off)
                P.tt(qf([[64, nh], [1, 64]]), ps.v([[64, nh], [1, 64]]), sv(rc, dims=[[1, nh], [0, 64]]), ALU.mult)
                P.tt(qf([[64, nh], [1, 64]]), qf([[64, nh], [1, 64]]), cv("cqg" if part < 2 else "ckg", [[0, nh], [1, 64]]), ALU.mult)
                cosv = cv("cos", [[0, nh], [1, 8]], add=tl["rb0"] * 8 + b * 8)
                sinv = cv("sin", [[0, nh], [1, 8]], add=tl["rb0"] * 8 + b * 8)
                x1 = qf([[64, nh], [1, 8]], 0); x2 = qf([[64, nh], [1, 8]], 8)
                r = lambda i: RT.v([[8, nh], [1, 8]], off=i * 64, key=i)
                P.tt(r(0), x1, cosv, ALU.mult); P.tt(r(1), x2, sinv, ALU.mult)
                P.tt(r(2), x2, cosv, ALU.mult); P.tt(r(3), x1, sinv, ALU.mult)
                P.tt(x1, r(0), r(1), ALU.subtract); P.tt(x2, r(2), r(3), ALU.add)
                P.copy(QB.v([[1, nh * 64]]), QF.v([[1, nh * 64]]))
                tb = nexttb()
                for h in range(nh):
                    P.tr(tb.v([[1, 128]], off=h * 128, np_=64), QB.v([[1, 64]], off=h * 64), IDB.v())
                if part < 2:
                    P.copy(QTx.v([[1, 1024]], off=part * 1024, np_=64, key="q%d" % part), tb.v([[1, 1024]], np_=64), eng="act")
                else:
                    if sample:
                        P.copy(KTN.v([[1, 512]]), tb.v([[1, 512]], np_=64), eng="act")
                        P.copy(VNB.v(), ps.v([[1, 256]], off=256))
                        P.copy(VO.v([[1, 256]]), ps.v([[1, 256]], off=256), eng="act")
                        for bb in range(16):
                            P.dma(o_swaks.d([[256, 8], [1, 256]], off=(bb * 128 + 120) * 256), QF.v([[1, 256]], p0=bb * 8, np_=8))
                            P.dma(o_swavs.d([[256, 8], [1, 256]], off=(bb * 128 + 120) * 256), VO.v([[1, 256]], p0=bb * 8, np_=8))
                    else:
                        P.copy(KTR.v([[512, 4], [1, 128]], off=slot * 128, key=slot), tb.v([[128, 4], [1, 128]], np_=64), eng="act")
                        P.copy(VVR.v([[128, 4], [64, 2], [1, 64]], off=slot * 512, key=slot), ps.v([[64, 4], [0, 2], [1, 64]], off=256))
                        if tl.get("last") and b == nb - 1:
                            P.dma(o_swakp.d([[256, 128], [1, 256]]), QF.v([[1, 256]]))
                            P.copy(VO.v([[1, 256]]), ps.v([[1, 256]], off=256), eng="act")
                            P.dma(o_swavp.d([[256, 128], [1, 256]]), VO.v([[1, 256]]))
                yield (("have", gb) if (part == 2 and not sample) else None)
            if not sample:
                yield ("need", gb - 1)
            for kv in range(4):
                qk = "q%d" % (kv // 2)
                qv = QTx.v([[1, 512]], off=kv * 512, np_=64, key=qk)
                pso = nextps(); psd = nextps()
                if not sample:
                    first = tl.get("first") and b == 0
                    plan = [((slot + 3) % 4, 2 if first else 1), (slot, 0)]
                    pts = []
                    for j, (sl, mi) in enumerate(plan):
                        ps = nextps()
                        P.mm(ps.v(), KTR.v([[1, 128]], off=(kv * 4 + sl) * 128, key=sl), qv)
                        pt = PTB[j].v([[1, 512]], off=(kv % 2) * 512, key=kv % 2)
                        P.act(pt, ps.v(), AF.Exp, scale=0.125)
                        pt4 = PTB[j].v([[128, 4], [1, 128]], off=(kv % 2) * 512, key=kv % 2)
                        P.tt(pt4, pt4, MASKB.v([[0, 4], [1, 128]], off=mi * 128), ALU.mult)
                        pts.append((pt, sl))
                    for j, (pt, sl) in enumerate(pts):
                        P.mm(pso.v(), VVR.v([[1, 128]], off=sl * 512 + kv * 128, key=sl), pt, j == 0, j == 1)
                    for j, (pt, sl) in enumerate(pts):
                        P.mm(psd.v(), ONES.v(), pt, j == 0, j == 1)
                    P.tt(RD.v([[128, 4], [1, 128]]), psd.v([[128, 4], [1, 128]]), ESINK.v([[1, 4], [0, 128]], off=kv * 4), ALU.add)
                    P.act(RD.v(), RD.v(), AF.Ln)
                    P.act(RD.v(), RD.v(), AF.Exp, scale=-1.0)
                    for hf_ in range(2):
                        P.tt(OTS.v([[128, 2], [1, 128]], off=xb * 1024 + kv * 256, p0=64 * hf_, np_=64, key="o%d" % xb),
                             pso.v([[256, 2], [1, 128]], off=128 * hf_, p0=64 * hf_, np_=64),
                             RD.v([[256, 2], [1, 128]], off=128 * hf_, p0=64 * hf_, np_=64), ALU.mult)
                else:
                    ps = nextps()
                    P.mm(ps.v([[32, 16], [8, 4], [1, 8]]), KTN.v([[1, 128]], off=kv * 128),
                         QTx.v([[8, 16], [128, 4], [1, 8]], off=kv * 512, np_=64, key=qk))
                    ptn = PTB[0].v([[1, 512]], off=(kv % 2) * 512, key=kv % 2)
                    P.act(ptn, ps.v(), AF.Exp, scale=0.125)
                    ptn4 = PTB[0].v([[32, 16], [8, 4], [1, 8]], off=(kv % 2) * 512, key=kv % 2)
                    P.tt(ptn4, ptn4, MASKB.v([[8, 16], [0, 4], [1, 8]], off=3 * 128), ALU.mult)
                    ps2 = nextps()
                    kt = kc_[kv // 2]
                    for bb in range(16):
                        P.mm(ps2.v([[8, 4], [1, 8]], off=bb * 32), kt.v([[1, 128]], off=((kv % 2) * 16 + bb) * 128, np_=64),
                             QTx.v([[128, 4], [1, 8]], off=kv * 512 + bb * 8, np_=64, key=qk))
                    ptc = PTB[1].v([[1, 512]], off=(kv % 2) * 512, key=kv % 2)
                    P.act(ptc, ps2.v(), AF.Exp, scale=0.125)
                    ptc3 = PTB[1].v([[8, 64], [1, 8]], off=(kv % 2) * 512, key=kv % 2)
                    P.tt(ptc3, ptc3, MSCB.v([[0, 64], [1, 8]]), ALU.mult)
                    for hp in range(2):
                        P.mm(pso.v(p0=64 * hp, np_=64), VNB.v([[1, 64]], off=kv * 64), ptn, True, False)
                        for bb in range(16):
                            P.mm(pso.v([[1, 32]], off=bb * 32, p0=64 * hp, np_=64), vc_.v([[1, 64]], off=bb * 256 + kv * 64),
                                 PTB[1].v([[1, 32]], off=(kv % 2) * 512 + bb * 32, key=kv % 2), False, bb == 15)
                    P.mm(psd.v(), ONES.v(), ptn, True, False)
                    P.mm(psd.v(), ONES.v(), ptc, False, True)
                    P.tt(RD.v([[32, 16], [8, 4], [1, 8]]), psd.v([[32, 16], [8, 4], [1, 8]]),
                         ESINK.v([[0, 16], [1, 4], [0, 8]], off=kv * 4), ALU.add)
                    P.act(RD.v(), RD.v(), AF.Ln)
                    P.act(RD.v(), RD.v(), AF.Exp, scale=-1.0)
                    for hf_ in range(2):
                        P.tt(OTS.v([[8, 16], [128, 2], [1, 8]], off=xb * 1024 + kv * 256, p0=64 * hf_, np_=64, key="o%d" % xb),
                             pso.v([[32, 16], [16, 2], [1, 8]], off=8 * hf_, p0=64 * hf_, np_=64),
                             RD.v([[32, 16], [16, 2], [1, 8]], off=8 * hf_, p0=64 * hf_, np_=64), ALU.mult)
                if kv % 2 == 1:
                    yield
        for half in range(2):
            wo = wget(tl.get('pair', -1), ("w_o", half), lambda: wl(b_wo.d([[8 * 1024, 128], [1024, 8], [1, 512]], off=half * 512, key=0)))
            for b in range(nb):
                xb = xb0 + b
                ps = nextps()
                for j in range(8):
                    P.mm(ps.v(), OTS.v([[1, 128]], off=xb * 1024 + j * 128, key="o%d" % xb),
                         wo.v([[1, 512]], off=j * 512), j == 0, j == 7)
                resid(tl, b, half, ps)
            yield

    def tile_gen(tl, src, dst):
        nb, xb0 = tl["nb"], tl["xb0"]
        for b in range(nb):
            P.dma(X.v([[1, D]], off=(xb0 + b) * D, key=xb0 + b), src(b), eng="pool")
        yield from mix_ab(tl)
        yield from mem_attn(tl, 0)
        yield from ffn(tl, 0)
        yield from swa(tl)
        yield from mem_attn(tl, 1)
        yield from ffn(tl, 1)
        if dst is not None:
            for b in range(nb):
                P.dma(dst(b), X.v([[1, D]], off=(xb0 + b) * D, key=xb0 + b), eng="pool")

    LAG = _HOOK.get("lag", 3)

    have = set()

    def run_pair(ga, gb_):
        gens = [g for g in (ga, gb_) if g is not None]
        if len(gens) == 1:
            for r in gens[0]:
                if isinstance(r, tuple) and r[0] == "have":
                    have.add(r[1])
            return
        a, b = gens
        a_alive = b_alive = True
        a_n = b_n = 0
        pend = None
        while a_alive or b_alive:
            if pend is not None and (pend in have or pend < 0 or not a_alive):
                pend = None
            adv_a = a_alive and (not b_alive or pend is not None or a_n - b_n < LAG)
            if adv_a:
                r = next(a, "END")
                a_n += 1
                if r == "END":
                    a_alive = False
                elif isinstance(r, tuple) and r[0] == "have":
                    have.add(r[1])
            else:
                r = next(b, "END")
                b_n += 1
                if r == "END":
                    b_alive = False
                elif isinstance(r, tuple):
                    if r[0] == "have":
                        have.add(r[1])
                    elif r[0] == "need" and r[1] >= 0 and r[1] not in have:
                        pend = r[1]

    memkv(0)
    memkv(1)
    NBLK = 2 + 4 * NT + 1
    tiles = [({"kind": "h", "W": HW, "nb": 2, "gb0": 0, "rb0": 0}, lambda b: d_xh.d([[D, 128], [1, D]], off=b * 128 * D), None)]
    for t in range(2 * NT - 1):
        tiles.append(({"kind": "p", "W": 256, "nb": 2, "gb0": 2 + 2 * t, "rb0": 2 + 2 * t, "first": t == 0},
                      (lambda b, t=t: d_xo.d([[D, 128], [1, D]], off=(t * 256 + b * 128) * D)),
                      (lambda b, t=t: o_y.d([[D, 128], [1, D]], off=(t * 256 + b * 128) * D))))
    for k in range(2):
        tb_ = 2 * (2 * NT - 1) + k
        tiles.append(({"kind": "p", "W": 128, "nb": 1, "gb0": 2 + tb_, "rb0": 2 + tb_, "first": False, "last": k == 1},
                      (lambda b, tb_=tb_: d_xo.d([[D, 128], [1, D]], off=tb_ * 128 * D)),
                      (lambda b, tb_=tb_: o_y.d([[D, 128], [1, D]], off=tb_ * 128 * D))))
    lanes = [[], []]
    for i, (tl, src, dst) in enumerate(tiles):
        j = i % 2
        tl.update({"c0": 256 * j, "xb0": 2 * j, "h": j, "pair": i // 2})
        lanes[j].append((tl, src, dst))
    gens = [None, None]
    lidx = [0, 0]
    cnt = [0, 0]
    alive = [True, True]

    def start(j):
        if lidx[j] < len(lanes[j]):
            tl, src, dst = lanes[j][lidx[j]]
            lidx[j] += 1
            gens[j] = tile_gen(tl, src, dst)
        else:
            gens[j] = None
            alive[j] = False

    start(0)
    start(1)
    next(gens[0])
    next(gens[1])
    cnt[0] += 1
    cnt[1] += 1
    for f in pre_late:
        f()
    pend = None
    while alive[0] or alive[1]:
        if pend is not None and (pend in have or pend < 0 or not alive[0]):
            pend = None
        adv0 = alive[0] and (not alive[1] or pend is not None or cnt[0] - cnt[1] < LAG)
        j = 0 if adv0 else 1
        r = next(gens[j], "END")
        cnt[j] += 1
        if r == "END":
            start(j)
        elif isinstance(r, tuple):
            if r[0] == "have":
                have.add(r[1])
            elif r[0] == "need" and j == 1 and r[1] >= 0 and r[1] not in have:
                pend = r[1]

    run_pair(tile_gen({"kind": "s", "W": 128, "nb": 1, "c0": 0, "xb0": 0, "h": 0, "gb0": 0, "rb0": NBLK - 1},
                      lambda b: d_xs.d([[D, 128], [1, D]]), lambda b: o_ys.d([[D, 128], [1, D]])), None)
    P.emit(stack)
    stack.close()
    return nc


def _img_k(w):
    K, N = w.shape
    return np.ascontiguousarray(w.reshape(K // 128, 128, N).transpose(1, 0, 2))


def _rep(row):
    row = np.asarray(row, np.float32).reshape(1, -1)
    return np.repeat(row, 128, axis=0)


_NC_CACHE = {}
_HOOK = {}


def kernel(**inp):
    f32 = np.float32
    A = {k: np.asarray(v) for k, v in inp.items()}
    xp = A["x_prompt"]
    B, SEQ, _ = xp.shape
    ncores = 2 * B
    HL = SEQ // 2
    NT = HL // TW
    xsamp = A["x_sample"]
    DB = xsamp.shape[0]
    assert xsamp.shape[1] == 8 and DB == 16 * ncores and HL % TW == 0
    CO, NCONST = const_layout(NT)
    NBLK = 2 + 4 * NT + 1

    shared = {}
    shared["w_in"] = _img_k(A["ab_w_in"][0])
    shared["w_out"] = _img_k(A["ab_w_out"][0])
    shared["w_qkv"] = _img_k(A["c_w_qkv"][0])
    shared["w_o"] = _img_k(A["c_w_o"][0])
    shared["m_wq"] = np.stack([_img_k(A["m_w_q"][l]) for l in range(2)])
    shared["m_wkv"] = np.stack([_img_k(A["m_w_kv"][l]) for l in range(2)])
    shared["m_wo"] = np.stack([_img_k(A["m_w_o"][l]) for l in range(2)])
    fup = []
    for l in range(2):
        img = _img_k(A["f_w_up"][l])
        g = img[:, :, :DFF].reshape(128, 8, NFC, 128)
        u = img[:, :, DFF:].reshape(128, 8, NFC, 128)
        fup.append(np.concatenate([g, u], -1).transpose(2, 0, 1, 3))
    shared["f_up"] = np.ascontiguousarray(np.stack(fup))
    shared["f_down"] = np.stack([_img_k(A["f_w_down"][l]) for l in range(2)])

    cbase = np.zeros((128, NCONST), f32)

    def put(c, name, arr):
        arr = np.asarray(arr, f32).reshape(128, -1)
        c[:, CO[name]:CO[name] + arr.shape[1]] = arr

    for l in range(2):
        put(cbase, "g_mix%d" % l, A["ln_mix"][l].reshape(8, 128).T)
        put(cbase, "g_mem%d" % l, A["ln_mem"][l].reshape(8, 128).T)
        put(cbase, "g_ffn%d" % l, A["ln_ffn"][l].reshape(8, 128).T)
        put(cbase, "g_memkv%d" % l, A["ln_memkv"][l].reshape(8, 128).T)
        put(cbase, "mqg%d" % l, A["m_q_gain"][l].reshape(128, 1))
        put(cbase, "mkg%d" % l, _rep(A["m_k_gain"][l]))
        put(cbase, "convw%d" % l, A["f_conv_w"][l].reshape(3, NFC, 128).transpose(2, 1, 0))
        put(cbase, "convb%d" % l, A["f_conv_b"][l].reshape(NFC, 128).T)
    put(cbase, "vgain", _rep(A["ab_v_gain"][0]))
    bs = A["ab_b_s"][0]
    put(cbase, "bs_p", _rep(bs.reshape(-1)))
    put(cbase, "bs_s", _rep(np.concatenate([np.tile(bs[g, :8], 16) for g in range(4)])))
    put(cbase, "pscale", A["ab_pool_scale"][0].reshape(4, 128).T)
    put(cbase, "cqg", _rep(A["c_q_gain"][0]))
    put(cbase, "ckg", _rep(A["c_k_gain"][0]))
    put(cbase, "sinks", _rep(A["c_sinks"][0]))
    ii = np.arange(128)
    put(cbase, "m_cur", (ii[:, None] <= ii[None, :]).astype(f32))
    put(cbase, "m_prev", (ii[:, None] > ii[None, :]).astype(f32))
    bb_, tt_ = ii // 8, ii % 8
    put(cbase, "m_sn", ((bb_[:, None] == bb_[None, :]) & (tt_[:, None] <= tt_[None, :])).astype(f32))
    msc = np.zeros((128, 8), f32)
    msc[:, :] = (ii[:, None] > np.arange(8)[None, :])
    put(cbase, "m_sc", msc)
    ws = A["ab_w_s"][0]
    c2h = np.zeros((128, NC2), f32)

    def put2(name, arr):
        arr = np.asarray(arr, f32).reshape(128, -1)
        c2h[:, C2[name]:C2[name] + arr.shape[1]] = arr

    put2("wsT_p", ws.transpose(2, 0, 1))
    wss = np.zeros((16, 8, 4, 16, 8), f32)
    small = ws[:, :8, :8].transpose(2, 0, 1)
    for b in range(16):
        wss[b, :, :, b, :] = small
    put2("wsT_s", wss)
    put2("poolw", A["ab_pool_w"][0].transpose(1, 0, 2))
    put2("ident", np.eye(128, dtype=f32))
    shared["consts2"] = c2h
    half_rot = 8
    inv = (np.float32(500000.0) ** (-np.arange(half_rot, dtype=f32) / np.float32(half_rot))).astype(f32)

    in_maps = []
    for c in range(ncores):
        seq, half = c // 2, c % 2
        start = half * HL
        b0 = c * 16
        m = dict(shared)
        m["xo"] = np.ascontiguousarray(xp[seq, start:start + HL])
        m["xh"] = np.ascontiguousarray(xp[seq, start - HW:start]) if half == 1 else np.zeros((HW, D), f32)
        m["xs"] = np.ascontiguousarray(xsamp[b0:b0 + 16].reshape(128, D))
        m["xm"] = np.ascontiguousarray(A["mem_prompt"][seq])
        cc = cbase.copy()
        put(cc, "flag", np.full((128, 1), float(half), f32))
        wnd = np.array([2, 4, 8, 16], f32)
        pos16 = np.arange(16, dtype=f32)
        if half == 0:
            ic = 1.0 / np.minimum(pos16[None, :] + 1.0, wnd[:, None])
        else:
            ic = np.repeat(1.0 / wnd[:, None], 16, axis=1)
        put(cc, "invcnt", _rep(ic.astype(f32).reshape(-1)))
        put(cc, "m_pf", (ii[:, None] > ii[None, :]).astype(f32) if half == 1 else np.zeros((128, 128), f32))
        pos = np.zeros((128, NBLK), np.int64)
        for rb in range(NBLK - 1):
            pos[:, rb] = start - HW + rb * 128 + ii
        pos[:, NBLK - 1] = PAST_LEN + (ii % 8)
        ang = pos.astype(f32)[:, :, None] * inv[None, None, :]
        put(cc, "cos", np.cos(ang).astype(f32))
        put(cc, "sin", np.sin(ang).astype(f32))
        m["consts"] = cc
        m["c_pool"] = np.ascontiguousarray(A["cache_pool"][0, b0:b0 + 16].reshape(16, 15, 4, 128).transpose(3, 2, 0, 1))
        m["c_swak"] = np.ascontiguousarray(A["cache_swa_k"][0, b0:b0 + 16].reshape(16, 128, 256))
        m["c_swav"] = np.ascontiguousarray(A["cache_swa_v"][0, b0:b0 + 16].reshape(16, 128, 256))
        m["c_swakT"] = np.ascontiguousarray(A["cache_swa_k"][0, b0:b0 + 16].transpose(3, 2, 0, 1))
        m["c_memkT"] = np.ascontiguousarray(A["cache_mem_k"][:, b0:b0 + 16].transpose(0, 1, 4, 3, 2))
        m["c_memv"] = np.ascontiguousarray(A["cache_mem_v"][:, b0:b0 + 16].reshape(2, 16, 256, 512))
        m["c_conv"] = np.ascontiguousarray(A["cache_ffn_conv"][:, b0:b0 + 16].reshape(2, 16, 2, NFC, 128).transpose(0, 4, 3, 1, 2))
        in_maps.append({k: np.ascontiguousarray(v, dtype=f32) for k, v in m.items()})

    if _HOOK.get("in_maps_only"):
        return in_maps, NT
    if NT not in _NC_CACHE:
        _NC_CACHE[NT] = build(NT)
    nc = _NC_CACHE[NT]
    if _HOOK.get("results") is not None:
        R = _HOOK["results"]
    else:
        res = run_bass_kernel_spmd(nc, in_maps, core_ids=list(range(ncores)))
        R = res.results

    y_p = np.zeros((B, SEQ, D), f32); y_s = np.zeros((DB, 8, D), f32)
    pool_p = np.zeros((1, B, 15, 512), f32); pool_s = np.zeros((1, DB, 15, 512), f32)
    chunk_v = np.zeros((1, DB, 8, 512), f32)
    swak_p = np.zeros((1, B, 128, 4, 64), f32); swav_p = np.zeros((1, B, 128, 4, 64), f32)
    swak_s = np.zeros((1, DB, 128, 4, 64), f32); swav_s = np.zeros((1, DB, 128, 4, 64), f32)
    memk = np.zeros((2, B, 256, 4, 128), f32); memv = np.zeros((2, B, 256, 4, 128), f32)
    conv_p = np.zeros((2, B, 2, DFF), f32); conv_s = np.zeros((2, DB, 2, DFF), f32)
    for c in range(ncores):
        seq, half = c // 2, c % 2
        start = half * HL
        b0 = c * 16
        r = R[c]
        y_p[seq, start:start + HL] = r["y"]
        y_s[b0:b0 + 16] = r["ys"].reshape(16, 8, D)
        pool_s[0, b0:b0 + 16] = r["o_pools"].transpose(2, 3, 1, 0).reshape(16, 15, 512)
        chunk_v[0, b0:b0 + 16] = r["o_chunkv"].reshape(16, 8, 512)
        swak_s[0, b0:b0 + 16] = r["o_swaks"].reshape(16, 128, 4, 64)
        swav_s[0, b0:b0 + 16] = r["o_swavs"].reshape(16, 128, 4, 64)
        for l in range(2):
            conv_s[l, b0:b0 + 16] = r["o_convs"][l].transpose(2, 3, 1, 0).reshape(16, 2, DFF)
        if half == 1:
            pool_p[0, seq] = r["o_poolp"].transpose(2, 1, 0).reshape(15, 512)
            swak_p[0, seq] = r["o_swakp"].reshape(128, 4, 64)
            swav_p[0, seq] = r["o_swavp"].reshape(128, 4, 64)
            for l in range(2):
                memk[l, seq] = r["o_memk"][l].reshape(256, 4, 128)
                memv[l, seq] = r["o_memv"][l].reshape(256, 4, 128)
                conv_p[l, seq] = r["o_convp"][l].transpose(2, 1, 0).reshape(2, DFF)
    return (y_p, y_s, pool_p, pool_s, chunk_v, swak_p, swav_p, swak_s, swav_s, memk, memv, conv_p, conv_s)
```
